# Optimizing a Trainium2 kernel written in Bass

```python
import jax, jax.numpy as jnp
from jax import lax
import numpy as np

D_MODEL = 1024
BATCH = 8
SEQ = 4096
DEPTH = 4

CTX_LEN = 256
GRID_W = 64
N_MIXERS = 2
N_ATTN_LAYERS = (DEPTH + 1) // 2
N_GLA_LAYERS = DEPTH // 2
HEAD_DIM = 128
N_HEADS = D_MODEL // HEAD_DIM
N_KV_HEADS = 2
Q_PER_KV = N_HEADS // N_KV_HEADS
Q_BLOCK = 128
ROPE_THETA = 10000.0
ROPE_AXIS_DIM = HEAD_DIM // 2
GLA_HEADS = 4
GLA_DK = D_MODEL // 2 // GLA_HEADS
GLA_DV = D_MODEL // GLA_HEADS
GLA_GATE_RANK = 16
GLA_GATE_TAU = 16.0
GLA_CHUNK = 64
D_FF = 2816
N_MOD = 9
EPS = 1e-6

kernel_name = 'hybrid_gqa_gla_macaron_dit'


def rms_norm(x, g):
    xf = x.astype(jnp.float32)
    y = xf * lax.rsqrt(jnp.mean(xf * xf, axis=-1, keepdims=True) + EPS)
    return (y * g.astype(jnp.float32)).astype(x.dtype)


def modulate(h, g, shift, scale):
    return rms_norm(h, g) * (1 + scale) + shift


def swiglu(u, w_in, w_out):
    gate, up = jnp.split(u @ w_in, 2, axis=-1)
    return (jax.nn.silu(gate) * up) @ w_out


def axial_rope_tables(rows):
    row = jnp.repeat(jnp.arange(rows, dtype=jnp.float32), GRID_W)
    col = jnp.tile(jnp.arange(GRID_W, dtype=jnp.float32), rows)
    inv = ROPE_THETA ** (-jnp.arange(0, ROPE_AXIS_DIM, 2, dtype=jnp.float32) / ROPE_AXIS_DIM)
    ang_r = row[:, None] * inv[None, :]
    ang_c = col[:, None] * inv[None, :]
    return (jnp.cos(ang_r), jnp.sin(ang_r), jnp.cos(ang_c), jnp.sin(ang_c))


def rotate(x, cos, sin):
    x1, x2 = jnp.split(x, 2, axis=-1)
    cos = cos[None, :, None, :]
    sin = sin[None, :, None, :]
    return jnp.concatenate([x1 * cos - x2 * sin, x2 * cos + x1 * sin], axis=-1)


def apply_axial_rope(x, rope):
    cos_r, sin_r, cos_c, sin_c = rope
    xr, xc = jnp.split(x, 2, axis=-1)
    return jnp.concatenate([rotate(xr, cos_r, sin_r), rotate(xc, cos_c, sin_c)], axis=-1).astype(x.dtype)


def attend(q, k, v):
    s = jnp.einsum('bqkgd,bskd->bkgqs', q, k).astype(jnp.float32) * (HEAD_DIM ** -0.5)
    p = jax.nn.softmax(s, axis=-1).astype(v.dtype)
    return jnp.einsum('bkgqs,bskd->bqkgd', p, v)


def gqa_axial_attention(uc, ux, w_qkv, q_gain, k_gain, w_o, rope, want_ctx):
    def project(u):
        b_, t_, _ = u.shape
        q, k, v = jnp.split(u @ w_qkv, [N_HEADS * HEAD_DIM, (N_HEADS + N_KV_HEADS) * HEAD_DIM], axis=-1)
        q = rms_norm(q.reshape(b_, t_, N_HEADS, HEAD_DIM), q_gain)
        k = rms_norm(k.reshape(b_, t_, N_KV_HEADS, HEAD_DIM), k_gain)
        v = v.reshape(b_, t_, N_KV_HEADS, HEAD_DIM)
        return q, k, v

    qc, kc, vc = project(uc)
    qx, kx, vx = project(ux)
    qx = apply_axial_rope(qx, rope)
    kx = apply_axial_rope(kx, rope)
    k_all = jnp.concatenate([kc, kx], axis=1)
    v_all = jnp.concatenate([vc, vx], axis=1)
    b_, s_ = qx.shape[:2]
    n_blk = s_ // Q_BLOCK
    q_blocks = jnp.moveaxis(qx.reshape(b_, n_blk, Q_BLOCK, N_KV_HEADS, Q_PER_KV, HEAD_DIM), 1, 0)
    ox = lax.map(lambda qb: attend(qb, k_all, v_all), q_blocks)
    ox = jnp.moveaxis(ox, 0, 1).reshape(b_, s_, N_HEADS * HEAD_DIM) @ w_o
    oc = None
    if want_ctx:
        l_ = qc.shape[1]
        oc = attend(qc.reshape(b_, l_, N_KV_HEADS, Q_PER_KV, HEAD_DIM), kc, vc)
        oc = oc.reshape(b_, l_, N_HEADS * HEAD_DIM) @ w_o
    return ox, oc


def gla_chunked(q, k, v, log_a, state0):
    b_, t_, h_, _ = q.shape
    dv = v.shape[-1]
    n = t_ // GLA_CHUNK

    def blocks(a):
        return jnp.moveaxis(a.astype(jnp.float32).reshape(b_, n, GLA_CHUNK, h_, a.shape[-1]), 1, 0)

    mask = jnp.tril(jnp.ones((GLA_CHUNK, GLA_CHUNK), dtype=bool))

    def step(state, inp):
        qc, kc, vc, gc = inp
        cum = jnp.cumsum(gc, axis=1)
        last = cum[:, -1]
        q_dec = qc * jnp.exp(cum)
        k_inv = kc * jnp.exp(-cum)
        k_end = kc * jnp.exp(last[:, None] - cum)
        scores = jnp.where(mask, jnp.einsum('bihd,bjhd->bhij', q_dec, k_inv), 0.0)
        o = jnp.einsum('bhij,bjhv->bihv', scores, vc) + jnp.einsum('bihd,bhdv->bihv', q_dec, state)
        state = jnp.exp(last)[..., None] * state + jnp.einsum('bjhd,bjhv->bhdv', k_end, vc)
        return state, o

    state, o = lax.scan(step, state0, (blocks(q), blocks(k), blocks(v), blocks(log_a)))
    o = jnp.moveaxis(o, 0, 1).reshape(b_, t_, h_, dv)
    return o, state


def bidirectional_gla(uc, ux, w_in, wa1, wa2, ba, o_gain, w_o, want_ctx):
    hk = GLA_HEADS * GLA_DK
    hv = GLA_HEADS * GLA_DV

    def project(u):
        b_, t_, _ = u.shape
        q, k, v, r = jnp.split(u @ w_in, [hk, 2 * hk, 2 * hk + hv], axis=-1)
        q = q.reshape(b_, t_, GLA_HEADS, GLA_DK) * (GLA_DK ** -0.5)
        k = k.reshape(b_, t_, GLA_HEADS, GLA_DK)
        v = v.reshape(b_, t_, GLA_HEADS, GLA_DV)
        g_fwd = jax.nn.log_sigmoid(((u @ wa1[0]) @ wa2[0] + ba[0]).astype(jnp.float32)) / GLA_GATE_TAU
        g_bwd = jax.nn.log_sigmoid(((u @ wa1[1]) @ wa2[1] + ba[1]).astype(jnp.float32)) / GLA_GATE_TAU
        return q, k, v, r, g_fwd.reshape(b_, t_, GLA_HEADS, GLA_DK), g_bwd.reshape(b_, t_, GLA_HEADS, GLA_DK)

    def flip(a):
        return jnp.flip(a, axis=1)

    qc, kc, vc, rc, gcf, gcb = project(uc)
    qx, kx, vx, rx, gxf, gxb = project(ux)
    s0 = jnp.zeros((uc.shape[0], GLA_HEADS, GLA_DK, GLA_DV), jnp.float32)
    oc_f, sc_f = gla_chunked(qc, kc, vc, gcf, s0)
    oc_b, sc_b = gla_chunked(flip(qc), flip(kc), flip(vc), flip(gcb), s0)
    ox_f, _ = gla_chunked(qx, kx, vx, gxf, sc_f)
    ox_b, _ = gla_chunked(flip(qx), flip(kx), flip(vx), flip(gxb), sc_b)

    def readout(o, r):
        b_, t_ = o.shape[:2]
        o = rms_norm(o, o_gain.reshape(GLA_HEADS, GLA_DV)).reshape(b_, t_, hv).astype(r.dtype)
        return (o * jax.nn.silu(r)) @ w_o

    ox = readout(ox_f + flip(ox_b), rx)
    oc = readout(oc_f + flip(oc_b), rc) if want_ctx else None
    return ox, oc


def setup_inputs(seed: int = 0) -> dict:
    key = jax.random.key(seed)
    ks = jax.random.split(key, 20)

    def normal(k, shape, scale=1.0):
        return jax.random.normal(k, shape, jnp.float32) * scale

    return {
        'x': normal(ks[0], (BATCH, SEQ, D_MODEL)),
        'c': normal(ks[1], (BATCH, D_MODEL)),
        'ctx': normal(ks[2], (BATCH, CTX_LEN, D_MODEL)),
        'c_ctx': normal(ks[3], (D_MODEL,)),
        'ada_w': normal(ks[4], (DEPTH, D_MODEL, N_MOD * D_MODEL), 0.5 * D_MODEL ** -0.5),
        'ada_b': normal(ks[5], (DEPTH, N_MOD * D_MODEL), 0.02),
        'norm_pre': 1.0 + normal(ks[6], (DEPTH, 3, D_MODEL), 0.02),
        'norm_post': 1.0 + normal(ks[7], (DEPTH, 3, D_MODEL), 0.02),
        'ffn_w_in': normal(ks[8], (DEPTH, 2, D_MODEL, 2 * D_FF), D_MODEL ** -0.5),
        'ffn_w_out': normal(ks[9], (DEPTH, 2, D_FF, D_MODEL), D_FF ** -0.5),
        'attn_w_qkv': normal(ks[10], (N_ATTN_LAYERS, D_MODEL, (N_HEADS + 2 * N_KV_HEADS) * HEAD_DIM), D_MODEL ** -0.5),
        'attn_q_gain': 1.0 + normal(ks[11], (N_ATTN_LAYERS, HEAD_DIM), 0.02),
        'attn_k_gain': 1.0 + normal(ks[12], (N_ATTN_LAYERS, HEAD_DIM), 0.02),
        'attn_w_o': normal(ks[13], (N_ATTN_LAYERS, N_HEADS * HEAD_DIM, D_MODEL), (N_HEADS * HEAD_DIM) ** -0.5),
        'gla_w_in': normal(ks[14], (N_GLA_LAYERS, D_MODEL, 2 * GLA_HEADS * GLA_DK + 2 * GLA_HEADS * GLA_DV), D_MODEL ** -0.5),
        'gla_wa1': normal(ks[15], (N_GLA_LAYERS, 2, D_MODEL, GLA_GATE_RANK), D_MODEL ** -0.5),
        'gla_wa2': normal(ks[16], (N_GLA_LAYERS, 2, GLA_GATE_RANK, GLA_HEADS * GLA_DK), GLA_GATE_RANK ** -0.5),
        'gla_ba': normal(ks[17], (N_GLA_LAYERS, 2, GLA_HEADS * GLA_DK), 0.1),
        'gla_o_gain': 1.0 + normal(ks[18], (N_GLA_LAYERS, GLA_HEADS * GLA_DV), 0.02),
        'gla_w_o': normal(ks[19], (N_GLA_LAYERS, GLA_HEADS * GLA_DV, D_MODEL), (GLA_HEADS * GLA_DV) ** -0.5),
    }


def reference(x, c, ctx, c_ctx, ada_w, ada_b, norm_pre, norm_post, ffn_w_in, ffn_w_out,
              attn_w_qkv, attn_q_gain, attn_k_gain, attn_w_o,
              gla_w_in, gla_wa1, gla_wa2, gla_ba, gla_o_gain, gla_w_o):
    b_, n_tok, _ = x.shape
    ROWS = n_tok // GRID_W
    rope = axial_rope_tables(ROWS)
    silu_c = jax.nn.silu(c)
    silu_cc = jax.nn.silu(c_ctx)
    hx, hc = x, ctx
    for i in range(DEPTH):
        last = i == DEPTH - 1
        mod_x = jnp.moveaxis((silu_c @ ada_w[i] + ada_b[i]).reshape(b_, N_MOD, 1, D_MODEL), 1, 0)
        mod_c = (silu_cc @ ada_w[i] + ada_b[i]).reshape(N_MOD, D_MODEL)
        g_pre, g_post = norm_pre[i], norm_post[i]

        def ffn_half(h, mod, j):
            u = modulate(h, g_pre[j], mod[3 * j], mod[3 * j + 1])
            y = swiglu(u, ffn_w_in[i, j // 2], ffn_w_out[i, j // 2])
            return h + 0.5 * mod[3 * j + 2] * rms_norm(y, g_post[j])

        hx = ffn_half(hx, mod_x, 0)
        hc = ffn_half(hc, mod_c, 0)
        ux = modulate(hx, g_pre[1], mod_x[3], mod_x[4])
        uc = modulate(hc, g_pre[1], mod_c[3], mod_c[4])
        m = i // N_MIXERS
        if i % N_MIXERS == 0:
            ox, oc = gqa_axial_attention(uc, ux, attn_w_qkv[m], attn_q_gain[m], attn_k_gain[m], attn_w_o[m],
                                         rope, not last)
        else:
            ox, oc = bidirectional_gla(uc, ux, gla_w_in[m], gla_wa1[m], gla_wa2[m], gla_ba[m], gla_o_gain[m],
                                       gla_w_o[m], not last)
        hx = hx + mod_x[5] * rms_norm(ox, g_post[1])
        hx = ffn_half(hx, mod_x, 2)
        if not last:
            hc = hc + mod_c[5] * rms_norm(oc, g_post[1])
            hc = ffn_half(hc, mod_c, 2)
    return hx
```

```python
import numpy as np
from contextlib import ExitStack
import concourse.bass as bass
import concourse.mybir as mybir
from concourse.bass_utils import run_bass_kernel_spmd

F32 = mybir.dt.float32
BF16 = mybir.dt.bfloat16
AF = mybir.ActivationFunctionType
ALU = mybir.AluOpType

D = 1024
KC = 8
NCTX = 256
NX = 4096
NTOK = NCTX + NX
DFF = 2816
FC = DFF // 128
EPS = 1e-6
NCORES = 8
HD = 128
NH = 8
NKV = 2


class Op:
    __slots__ = ("eng", "fn", "waits", "signal", "count", "group", "is_dma")

    def __init__(self, eng, fn, group=None):
        self.eng = eng
        self.fn = fn
        self.waits = []
        self.signal = False
        self.count = None
        self.group = group
        self.is_dma = group is not None


class Sched:
    ENGS = ("pe", "act", "dve", "pool", "sp")

    def __init__(self, nc):
        self.nc = nc
        self.ops = {e: [] for e in self.ENGS}
        self.res = {}
        self.groups = {}
        self.rot = {}

    def nxt(self, name, n):
        i = self.rot.get(name, 0)
        self.rot[name] = i + 1
        return i % n

    def op(self, eng, fn, reads=(), writes=(), group=None):
        o = Op(eng, fn, group)
        deps = {}
        for r in reads:
            ent = self.res.get(r)
            if ent is not None and ent[0] is not None:
                deps[id(ent[0])] = (ent[0], True)
        for r in writes:
            ent = self.res.get(r)
            if ent is not None:
                if ent[0] is not None:
                    deps[id(ent[0])] = (ent[0], True)
                for rd in ent[1]:
                    if id(rd) not in deps:
                        deps[id(rd)] = (rd, False)
        for d, hard in deps.values():
            if d is o:
                continue
            if (not d.is_dma) and (not o.is_dma) and d.eng == eng:
                if eng == "pe" or not hard:
                    continue
            d.signal = True
            o.waits.append(d)
        for r in reads:
            ent = self.res.setdefault(r, [None, []])
            ent[1].append(o)
        for r in writes:
            self.res[r] = [o, []]
        if o.is_dma:
            o.signal = True
            self.groups.setdefault(group, None)
        self.ops[eng].append(o)
        return o

    def mm(self, out, lhsT, rhs, start, stop, reads, writes):
        return self.op("pe", lambda e: e.matmul(out, lhsT, rhs, start=start, stop=stop), reads, writes)

    def tr(self, out, in_, ident, reads, writes):
        return self.op("pe", lambda e: e.transpose(out, in_, ident), reads, writes)

    def actf(self, out, in_, func, reads, writes, bias=0.0, scale=1.0):
        return self.op("act", lambda e: e.activation(out, in_, func, bias=bias, scale=scale), reads, writes)

    def tt(self, out, a, b, op, reads, writes, eng="dve"):
        return self.op(eng, lambda e: e.tensor_tensor(out, a, b, op), reads, writes)

    def ts(self, out, a, s1, s2, op0, op1, reads, writes, eng="dve"):
        return self.op(eng, lambda e: e.tensor_scalar(out, a, s1, s2, op0, op1), reads, writes)

    def stt(self, out, in0, scalar, in1, op0, op1, reads, writes, eng="dve"):
        return self.op(eng, lambda e: e.scalar_tensor_tensor(out, in0, scalar, in1, op0, op1), reads, writes)

    def recip(self, out, in_, reads, writes):
        return self.op("dve", lambda e: e.reciprocal(out, in_), reads, writes)

    def copy(self, out, in_, reads, writes, eng="dve"):
        return self.op(eng, lambda e: e.tensor_copy(out, in_), reads, writes)

    def memset(self, ap, val, writes, eng="dve"):
        return self.op(eng, lambda e: e.memset(ap, val), (), writes)

    def dma(self, queue, group, out, in_, reads=(), writes=()):
        return self.op(queue, lambda e: e.dma_start(out=out, in_=in_), reads, writes, group=group)

    def barrier(self):
        lasts = []
        for e in self.ENGS:
            lc = None
            for o in self.ops[e]:
                if not o.is_dma and o.fn is not None:
                    lc = o
            if lc is not None:
                lasts.append(lc)
        lastg = {}
        for e in self.ENGS:
            for o in self.ops[e]:
                if o.is_dma:
                    lastg[o.group] = o
        lasts += list(lastg.values())
        for e in self.ENGS:
            o = Op(e, None)
            for d in lasts:
                if (not d.is_dma) and d.eng == e:
                    continue
                d.signal = True
                o.waits.append(d)
            self.ops[e].append(o)
        self.res = {}

    def emit(self, final_groups=()):
        nc = self.nc
        with ExitStack() as es:
            esem = {e: es.enter_context(nc.semaphore("s_" + e)) for e in self.ENGS}
            gsem = {g: es.enter_context(nc.semaphore("g_" + str(g))) for g in self.groups}
            gcount = {g: 0 for g in self.groups}
            gq = {}
            for e in self.ENGS:
                c = 0
                for o in self.ops[e]:
                    if o.is_dma:
                        assert gq.setdefault(o.group, e) == e, ("group on two queues", o.group)
                        gcount[o.group] += 16
                        o.count = gcount[o.group]
                    elif o.signal:
                        c += 1
                        o.count = c
            block = es.enter_context(nc.Block())

            def run(e, eng):
                known = {}
                for o in self.ops[e]:
                    for d in o.waits:
                        if d.is_dma:
                            sem, key = gsem[d.group], ("g", d.group)
                        else:
                            sem, key = esem[d.eng], ("e", d.eng)
                        if known.get(key, 0) >= d.count:
                            continue
                        known[key] = d.count
                        eng.wait_ge(sem, d.count)
                    if o.fn is None:
                        continue
                    ins = o.fn(eng)
                    if o.is_dma:
                        ins.then_inc(gsem[o.group], 16)
                    elif o.signal:
                        ins.then_inc(esem[e], 1)
                if e == "sp":
                    for g in final_groups:
                        if gcount.get(g, 0) > 0:
                            eng.wait_ge(gsem[g], gcount[g])

            @block.tensor
            def _(eng):
                run("pe", eng)

            @block.scalar
            def _(eng):
                run("act", eng)

            @block.vector
            def _(eng):
                run("dve", eng)

            @block.gpsimd
            def _(eng):
                run("pool", eng)

            @block.sync
            def _(eng):
                run("sp", eng)


class Ctx:
    def __init__(self):
        self.nc = bass.Bass("TRN2", target_bir_lowering=False)
        self.es = ExitStack()
        self.S = Sched(self.nc)
        self.out_groups = []

    def din(self, name, shape, dt=F32):
        return self.nc.dram_tensor(name, list(shape), dt, kind="ExternalInput").ap()

    def dout(self, name, shape, dt=F32):
        return self.nc.dram_tensor(name, list(shape), dt, kind="ExternalOutput").ap()

    def dscr(self, name, shape, dt=F32):
        return self.nc.dram_tensor(name, list(shape), dt).ap()

    def sb(self, name, shape, dt):
        return self.es.enter_context(self.nc.sbuf_tensor(name, list(shape), dt))

    def psum_banks(self):
        return [self.es.enter_context(self.nc.psum_tensor("ps%d" % i, [128, 512], F32)) for i in range(8)]

    def finish(self):
        self.S.emit(final_groups=self.out_groups)
        self.es.close()
        return self.nc


def tiles_for(ntok_from, with_ctx, tmax):
    tl = []
    if with_ctx:
        tl.append((0, NCTX, 1))
    t = NCTX
    while t < NTOK:
        tl.append((t, tmax, 0))
        t += tmax
    return tl


def ht_view(ap):
    return ap.rearrange("(kc p) t -> p kc t", p=128)


def load_consts_mod(C, modT, gpre, gpost, j, coef):
    S = C.S
    modsb = C.sb("modsb", [128, 2, 72], F32)
    gp = C.sb("gp", [128, 8], F32)
    gq = C.sb("gq", [128, 8], F32)
    A = C.sb("modA", [128, 2, 8], F32)
    Cg = C.sb("modC", [128, 2, 8], F32)
    S.dma("sp", "g_mod", modsb[:], modT, writes=["modsb"])
    S.dma("sp", "g_gp", gp[:], gpre, writes=["gp"])
    S.dma("sp", "g_gq", gq[:], gpost, writes=["gq"])
    for m in range(2):
        sc = modsb[:, m, (3 * j + 1) * 8:(3 * j + 2) * 8]
        gt = modsb[:, m, (3 * j + 2) * 8:(3 * j + 3) * 8]
        S.stt(A[:, m, :], sc, 1.0, gp[:], ALU.add, ALU.mult, reads=["modsb", "gp"], writes=[("A", m)])
        S.stt(Cg[:, m, :], gt, coef, gq[:], ALU.mult, ALU.mult, reads=["modsb", "gq"], writes=[("C", m)])
    B = lambda m, kc: modsb[:, m, 3 * j * 8 + kc:3 * j * 8 + kc + 1]
    Af = lambda m, kc: A[:, m, kc:kc + 1]
    Cf = lambda m, kc: Cg[:, m, kc:kc + 1]
    return Af, B, Cf


class Common:
    def __init__(self, C, TM):
        self.C = C
        self.TM = TM
        self.h = C.sb("h", [128, KC, TM], F32)
        self.u = C.sb("u", [128, KC, TM], BF16)
        self.sq = [C.sb("sq%d" % i, [128, 512], BF16) for i in range(2)]
        self.tmp = [C.sb("tmp%d" % i, [128, 512], F32) for i in range(2)]
        self.rstd = C.sb("rstd", [128, TM], F32)
        self.ones = C.sb("ones", [128, 128], BF16)
        C.S.memset(self.ones[:], 1.0, writes=["ones"])

    def load_h(self, HT_in, t0, T):
        S = self.C.S
        S.dma("sp", "g_h", self.h[:, :, 0:T], ht_view(HT_in)[:, :, t0:t0 + T],
              writes=[("h", kc, s) for kc in range(KC) for s in range(2)])

    def rms_from_psum_ss(self, ps_ss, W, dst, key, ndim):
        S = self.C.S
        S.actf(dst, ps_ss, AF.Sqrt, reads=[key[0]], writes=[key[1]], bias=EPS, scale=1.0 / ndim)
        S.recip(dst, dst, reads=[key[1]], writes=[key[1]])

    def prenorm_mod(self, T, m, Af, Bf, ps, bank_ss):
        S = self.C.S
        W = min(T, 512)
        NS = T // W
        for s in range(NS):
            cs = slice(s * W, (s + 1) * W)
            b = bank_ss[s]
            for kc in range(KC):
                i = S.nxt("sq", 2)
                S.actf(self.sq[i][:, :W], self.h[:, kc, cs], AF.Square, reads=[("h", kc, s)], writes=[("sq", i)])
                S.mm(ps[b][:, :W], self.ones[:], self.sq[i][:, :W], kc == 0, kc == KC - 1,
                     reads=["ones", ("sq", i)], writes=[("ps", b)])
            self.rms_from_psum_ss(ps[b][:, :W], W, self.rstd[:, cs], (("ps", b), ("rstd", s)), D)
        for s in range(NS):
            cs = slice(s * W, (s + 1) * W)
            for kc in range(KC):
                i = S.nxt("tmp", 2)
                S.tt(self.tmp[i][:, :W], self.h[:, kc, cs], self.rstd[:, cs], ALU.mult,
                     reads=[("h", kc, s), ("rstd", s)], writes=[("tmp", i)])
                S.actf(self.u[:, kc, cs], self.tmp[i][:, :W], AF.Identity, reads=[("tmp", i), ("A", m), "modsb"],
                       writes=[("u", kc, s)], bias=Bf(m, kc), scale=Af(m, kc))

    def postnorm_residual_store(self, y, rstd_y, T, m, Cf, HT_out, t0):
        S = self.C.S
        W = min(T, 512)
        NS = T // W
        for s in range(NS):
            cs = slice(s * W, (s + 1) * W)
            for kc in range(KC):
                i = S.nxt("tmp", 2)
                S.tt(self.tmp[i][:, :W], y[:, kc, cs], rstd_y[:, cs], ALU.mult,
                     reads=[("y", kc, s), ("rstdy", s)], writes=[("tmp", i)])
                S.stt(self.h[:, kc, cs], self.tmp[i][:, :W], Cf(m, kc), self.h[:, kc, cs], ALU.mult, ALU.add,
                      reads=[("tmp", i), ("C", m), ("h", kc, s)], writes=[("h", kc, s)])
        S.dma("sp", "g_hst", ht_view(HT_out)[:, :, t0:t0 + T], self.h[:, :, 0:T],
              reads=[("h", kc, s) for kc in range(KC) for s in range(2)], writes=[("HTout", t0)])


def prog_ffn(j, with_ctx):
    C = Ctx()
    S = C.S
    TM = 1024
    HT_in = C.din("ht_in", [D, NTOK])
    w_in = C.din("w_in", [D, 2 * DFF])
    w_out = C.din("w_out", [DFF, D])
    modT = C.din("modT", [128, 2, 72])
    gpre = C.din("gpre", [128, 8])
    gpost = C.din("gpost", [128, 8])
    HT_out = C.dout("ht_out", [D, NTOK])
    C.out_groups.append("g_hst")
    if not with_ctx:
        C.out_groups.append("g_cp")
    ps = C.psum_banks()
    Af, Bf, Cf = load_consts_mod(C, modT, gpre, gpost, j, 0.5)
    cm = Common(C, TM)
    actb = C.sb("actb", [128, FC, TM], BF16)
    y = C.sb("y", [128, KC, TM], F32)
    rstdy = C.sb("rstdy", [128, TM], F32)
    win = [C.sb("win%d" % i, [128, KC, 512], BF16) for i in range(3)]
    wo = [C.sb("wo%d" % i, [128, FC, 256], BF16) for i in range(2)]
    sg = [C.sb("sg%d" % i, [128, 512], BF16) for i in range(2)]
    w_in_v = w_in.rearrange("(kc p) f -> p kc f", p=128)
    w_out_v = w_out.rearrange("(fc p) d -> p fc d", p=128)

    if not with_ctx:
        S.dma("sp", "g_cp0", cm.h[:, :, 0:NCTX], ht_view(HT_in)[:, :, 0:NCTX],
              writes=[("h", kc, s) for kc in range(KC) for s in range(2)])
        S.dma("sp", "g_cp", ht_view(HT_out)[:, :, 0:NCTX], cm.h[:, :, 0:NCTX],
              reads=[("h", kc, s) for kc in range(KC) for s in range(2)], writes=[("HTout", 0)])

    for (t0, T, m) in tiles_for(0, with_ctx, TM):
        W = min(T, 512)
        NS = T // W
        cm.load_h(HT_in, t0, T)
        cm.prenorm_mod(T, m, Af, Bf, ps, (6, 7))
        for g in range(11):
            sl = S.nxt("win", 3)
            S.dma("pool", "g_win%d" % sl, win[sl][:, :, 0:256], w_in_v[:, :, g * 256:(g + 1) * 256], writes=[("win", sl)])
            S.dma("pool", "g_win%d" % sl, win[sl][:, :, 256:512], w_in_v[:, :, DFF + g * 256:DFF + (g + 1) * 256],
                  writes=[("win", sl)])
            for f2 in range(2):
                f = g * 2 + f2
                for s in range(NS):
                    cs = slice(s * W, (s + 1) * W)
                    pr = S.nxt("gu", 3)
                    bg, bu = 2 * pr, 2 * pr + 1
                    for kc in range(KC):
                        S.mm(ps[bg][:, :W], win[sl][:, kc, f2 * 128:(f2 + 1) * 128], cm.u[:, kc, cs], kc == 0, kc == KC - 1,
                             reads=[("win", sl), ("u", kc, s)], writes=[("ps", bg)])
                    for kc in range(KC):
                        S.mm(ps[bu][:, :W], win[sl][:, kc, 256 + f2 * 128:256 + (f2 + 1) * 128], cm.u[:, kc, cs], kc == 0,
                             kc == KC - 1, reads=[("win", sl), ("u", kc, s)], writes=[("ps", bu)])
                    i = S.nxt("sg", 2)
                    S.actf(sg[i][:, :W], ps[bg][:, :W], AF.Silu, reads=[("ps", bg)], writes=[("sg", i)])
                    S.tt(actb[:, f, cs], sg[i][:, :W], ps[bu][:, :W], ALU.mult, reads=[("sg", i), ("ps", bu)],
                         writes=[("actb", f, s)])
        for q in range(4):
            sl = S.nxt("wo", 2)
            S.dma("pool", "g_wo%d" % sl, wo[sl][:], w_out_v[:, :, q * 256:(q + 1) * 256], writes=[("wo", sl)])
            for d2 in range(2):
                dc = q * 2 + d2
                for s in range(NS):
                    cs = slice(s * W, (s + 1) * W)
                    b = S.nxt("yb", 6)
                    for f in range(FC):
                        S.mm(ps[b][:, :W], wo[sl][:, f, d2 * 128:(d2 + 1) * 128], actb[:, f, cs], f == 0, f == FC - 1,
                             reads=[("wo", sl), ("actb", f, s)], writes=[("ps", b)])
                    S.actf(y[:, dc, cs], ps[b][:, :W], AF.Copy, reads=[("ps", b)], writes=[("y", dc, s)])
                    i = S.nxt("sq", 2)
                    S.actf(cm.sq[i][:, :W], ps[b][:, :W], AF.Square, reads=[("ps", b)], writes=[("sq", i)])
                    S.mm(ps[6 + s][:, :W], cm.ones[:], cm.sq[i][:, :W], dc == 0, dc == KC - 1,
                         reads=["ones", ("sq", i)], writes=[("ps", 6 + s)])
        for s in range(NS):
            cs = slice(s * W, (s + 1) * W)
            cm.rms_from_psum_ss(ps[6 + s][:, :W], W, rstdy[:, cs], (("ps", 6 + s), ("rstdy", s)), D)
        cm.postnorm_residual_store(y, rstdy, T, m, Cf, HT_out, t0)
    return C.finish()


def prog_attn():
    C = Ctx()
    S = C.S
    TM = 512
    HT_in = C.din("ht_in", [D, NTOK])
    w_qkv = C.din("w_qkv", [D, 1536])
    w_o = C.din("w_o", [D, D])
    modT = C.din("modT", [128, 2, 72])
    gpre = C.din("gpre", [128, 8])
    gpost = C.din("gpost", [128, 8])
    qkg = C.din("qkg", [128, 2])
    cosT = C.din("cosT", [128, NTOK])
    sinT = C.din("sinT", [128, NTOK])
    RTd = C.din("RT", [128, 128])
    HT_out = C.dout("ht_out", [D, NTOK])
    C.out_groups.append("g_hst")
    ps = C.psum_banks()
    Af, Bf, Cf = load_consts_mod(C, modT, gpre, gpost, 1, 1.0)
    cm = Common(C, TM)
    QT = C.sb("QT", [128, NH, NTOK], BF16)
    KT = C.sb("KT", [128, NKV, NTOK], BF16)
    V = C.sb("V", [128, NTOK // 128, 256], BF16)
    wbuf = C.sb("wbuf", [128, KC, 1536], BF16)
    RT = C.sb("RTsb", [128, 128], BF16)
    gains = C.sb("gains", [128, 2], F32)
    cs_t = C.sb("cs_t", [128, 2, 512], F32)
    qb = [C.sb("qb%d" % i, [128, 512], BF16) for i in range(2)]
    rq = [C.sb("rq%d" % i, [128, 512], F32) for i in range(2)]
    t1 = [C.sb("t1_%d" % i, [128, 512], F32) for i in range(2)]
    t2 = [C.sb("t2_%d" % i, [128, 512], F32) for i in range(2)]
    PT = [C.sb("PT%d" % i, [128, 512], BF16) for i in range(4)]
    OT = C.sb("OT", [128, NH, 512], BF16)
    rec = rq
    y = C.sb("y", [128, KC, TM], F32)
    rstdy = C.sb("rstdy", [128, TM], F32)
    S.dma("pool", "g_w", wbuf[:], w_qkv.rearrange("(kc p) f -> p kc f", p=128), writes=["wbuf"])
    S.dma("pool", "g_rt", RT[:], RTd, writes=["RT"])
    S.dma("sp", "g_gn", gains[:], qkg, writes=["gains"])
    tiles = tiles_for(0, True, TM)
    for ti, (t0, T, m) in enumerate(tiles):
        cm.load_h(HT_in, t0, T)
        cm.prenorm_mod(T, m, Af, Bf, ps, (7, 7))
        S.dma("sp", "g_cs", cs_t[:, 0, 0:T], cosT[:, t0:t0 + T], writes=["cs_t"])
        S.dma("sp", "g_cs", cs_t[:, 1, 0:T], sinT[:, t0:t0 + T], writes=["cs_t"])
        for hh in range(NH + NKV):
            b = S.nxt("a1raw", 3)
            for kc in range(KC):
                S.mm(ps[b][:, :T], wbuf[:, kc, hh * 128:(hh + 1) * 128], cm.u[:, kc, 0:T], kc == 0, kc == KC - 1,
                     reads=["wbuf", ("u", kc, 0)], writes=[("ps", b)])
            gi = 0 if hh < NH else 1
            i = S.nxt("qb", 2)
            S.actf(qb[i][:, :T], ps[b][:, :T], AF.Identity, reads=[("ps", b), "gains"], writes=[("qb", i)],
                   scale=gains[:, gi:gi + 1])
            k = S.nxt("sq", 2)
            S.actf(cm.sq[k][:, :T], ps[b][:, :T], AF.Square, reads=[("ps", b)], writes=[("sq", k)])
            bs = 3 + S.nxt("a1ss", 2)
            br = 5 + S.nxt("a1rot", 2)
            S.mm(ps[bs][:, :T], cm.ones[:], cm.sq[k][:, :T], True, True, reads=["ones", ("sq", k)], writes=[("ps", bs)])
            S.mm(ps[br][:, :T], RT[:], qb[i][:, :T], True, True, reads=["RT", ("qb", i)], writes=[("ps", br)])
            S.actf(rq[i][:, :T], ps[bs][:, :T], AF.Sqrt, reads=[("ps", bs)], writes=[("rq", i)], bias=EPS, scale=1.0 / HD)
            S.recip(rq[i][:, :T], rq[i][:, :T], reads=[("rq", i)], writes=[("rq", i)])
            S.tt(t1[i][:, :T], qb[i][:, :T], cs_t[:, 0, 0:T], ALU.mult, reads=[("qb", i), "cs_t"], writes=[("t1", i)])
            S.tt(t2[i][:, :T], ps[br][:, :T], cs_t[:, 1, 0:T], ALU.mult, reads=[("ps", br), "cs_t"], writes=[("t2", i)])
            S.tt(t1[i][:, :T], t1[i][:, :T], t2[i][:, :T], ALU.add, reads=[("t1", i), ("t2", i)], writes=[("t1", i)])
            if hh < NH:
                dst, key = QT[:, hh, t0:t0 + T], ("QT", hh, ti)
            else:
                dst, key = KT[:, hh - NH, t0:t0 + T], ("KT", hh - NH, ti)
            S.tt(dst, t1[i][:, :T], rq[i][:, :T], ALU.mult, reads=[("t1", i), ("rq", i)], writes=[key])
        for blk in range(T // 128):
            for kc in range(KC):
                S.mm(ps[7][:, 0:256], cm.u[:, kc, blk * 128:(blk + 1) * 128], wbuf[:, kc, 1280:1536], kc == 0, kc == KC - 1,
                     reads=["wbuf", ("u", kc, 0)], writes=[("ps", 7)])
            S.actf(V[:, t0 // 128 + blk, :], ps[7][:, 0:256], AF.Copy, reads=[("ps", 7)], writes=[("V", ti, blk)])
    S.dma("pool", "g_w", wbuf[:, :, 0:D], w_o.rearrange("(kc p) f -> p kc f", p=128), reads=[], writes=["wbuf"])

    def kt_tile(kb):
        return 0 if kb < 2 else 1 + (kb - 2) // 4

    def v_key(kb):
        return ("V", 0, kb) if kb < 2 else ("V", 1 + (kb - 2) // 4, (kb - 2) % 4)

    sm_scale = float(HD) ** -0.5
    for ti, (t0, T, m) in enumerate(tiles):
        nkb = 2 if m == 1 else NTOK // 128
        for hh in range(NH):
            g = hh // (NH // NKV)
            bo = 4 + 2 * (hh % 2)
            bl = bo + 1
            for kb in range(nkb):
                bsn = S.nxt("a2s", 4)
                S.mm(ps[bsn][:, :T], KT[:, g, kb * 128:(kb + 1) * 128], QT[:, hh, t0:t0 + T], True, True,
                     reads=[("KT", g, kt_tile(kb)), ("QT", hh, ti)], writes=[("ps", bsn)])
                pi = S.nxt("PT", 4)
                S.actf(PT[pi][:, :T], ps[bsn][:, :T], AF.Exp, reads=[("ps", bsn)], writes=[("PT", pi)], scale=sm_scale)
                S.mm(ps[bo][:, :T], V[:, kb, g * 128:(g + 1) * 128], PT[pi][:, :T], kb == 0, kb == nkb - 1,
                     reads=[v_key(kb), ("PT", pi)], writes=[("ps", bo)])
                S.mm(ps[bl][:, :T], cm.ones[:], PT[pi][:, :T], kb == 0, kb == nkb - 1,
                     reads=["ones", ("PT", pi)], writes=[("ps", bl)])
            ri = S.nxt("rec", 2)
            S.recip(rec[ri][:, :T], ps[bl][:, :T], reads=[("ps", bl)], writes=[("rq", ri)])
            S.tt(OT[:, hh, 0:T], ps[bo][:, :T], rec[ri][:, :T], ALU.mult, reads=[("ps", bo), ("rq", ri)],
                 writes=[("OT", hh)])
        for dc in range(KC):
            b = S.nxt("a2y", 4)
            for hh in range(NH):
                S.mm(ps[b][:, :T], wbuf[:, hh, dc * 128:(dc + 1) * 128], OT[:, hh, 0:T], hh == 0, hh == NH - 1,
                     reads=["wbuf", ("OT", hh)], writes=[("ps", b)])
            S.actf(y[:, dc, 0:T], ps[b][:, :T], AF.Copy, reads=[("ps", b)], writes=[("y", dc, 0)])
            k = S.nxt("sq", 2)
            S.actf(cm.sq[k][:, :T], ps[b][:, :T], AF.Square, reads=[("ps", b)], writes=[("sq", k)])
            S.mm(ps[4][:, :T], cm.ones[:], cm.sq[k][:, :T], dc == 0, dc == KC - 1,
                 reads=["ones", ("sq", k)], writes=[("ps", 4)])
        cm.rms_from_psum_ss(ps[4][:, :T], T, rstdy[:, 0:T], (("ps", 4), ("rstdy", 0)), D)
        cm.load_h(HT_in, t0, T)
        cm.postnorm_residual_store(y, rstdy, T, m, Cf, HT_out, t0)
    return C.finish()


def rope_consts():
    inv = 10000.0 ** (-np.arange(0, 64, 2, dtype=np.float32) / 64.0)
    t = np.arange(NX)
    row = (t // 64).astype(np.float32)
    col = (t % 64).astype(np.float32)
    cosT = np.ones((128, NTOK), np.float32)
    sinT = np.zeros((128, NTOK), np.float32)
    RT = np.zeros((128, 128), np.float32)
    for p in range(128):
        half, within = p // 64, p % 64
        part, f = within // 32, within % 32
        pos = row if half == 0 else col
        ang = (pos * inv[f]).astype(np.float32)
        cosT[p, NCTX:] = np.cos(ang)
        sinT[p, NCTX:] = np.sin(ang)
        if part == 0:
            RT[p + 32, p] = -1.0
        else:
            RT[p - 32, p] = 1.0
    return cosT, sinT, RT


GH = 4
GDK = 128
GDV = 256
NBLK = NTOK // 128


def gla_consts():
    j = np.arange(128)[:, None]
    i = np.arange(128)[None, :]
    same = (j // 64) == (i // 64)
    sc = np.float32(-1.0 / 16.0)
    ind = ((np.arange(128)[:, None] // 64) == np.arange(2)[None, :]).astype(np.float32)
    out = {}
    for d, (le, gt) in enumerate([(j <= i, j > i), (j >= i, j < i)]):
        L2 = (same & le).astype(np.float32)
        U2 = (same & gt).astype(np.float32)
        out["LI%d" % d] = np.ascontiguousarray(np.concatenate([L2 * sc, ind * sc], axis=1))
        out["U2%d" % d] = np.ascontiguousarray(U2 * sc)
        out["M2%d" % d] = np.ascontiguousarray(L2)
    return out


GTM = 256


def prog_gla(with_ctx):
    C = Ctx()
    S = C.S
    TM = GTM
    io = {}
    io["ht_in"] = C.din("ht_in", [D, NTOK])
    io["w_in"] = C.din("w_in", [D, 3072])
    io["wa1"] = C.din("wa1", [2, D, 16])
    io["wa2a"] = C.din("wa2a", [2, 33, 512])
    io["w_o"] = C.din("w_o", [D, D])
    io["ogain"] = C.din("ogain", [128, 8])
    modT = C.din("modT", [128, 2, 72])
    gpre = C.din("gpre", [128, 8])
    gpost = C.din("gpost", [128, 8])
    for n in ("LI0", "LI1", "U20", "U21", "M20", "M21"):
        io[n] = C.din(n, [128, 130 if n.startswith("LI") else 128])
    io["ht_out"] = C.dout("ht_out", [D, NTOK])
    io["qT"] = C.dscr("qT", [128, GH, NTOK], BF16)
    io["kT"] = C.dscr("kT", [128, GH, NTOK], BF16)
    io["ktok"] = C.dscr("ktok", [NTOK, 512], BF16)
    io["vtok"] = C.dscr("vtok", [NTOK, 1024], BF16)
    io["gsp"] = C.dscr("gsp", [2, NTOK, 512], F32)
    io["srT"] = C.dscr("srT", [128, 8, NTOK], BF16)
    io["oT"] = C.dscr("oT", [2, 128, 8, NTOK], F32)
    ps = C.psum_banks()
    Af, Bf, Cf = load_consts_mod(C, modT, gpre, gpost, 1, 1.0)
    cm = Common(C, TM)
    wbuf = C.sb("wbuf", [128, KC, 3072], BF16)
    gla_p1(C, cm, ps, Af, Bf, io, wbuf, TM)
    S.barrier()
    gla_p2(C, ps, io)
    S.barrier()
    gla_p3(C, cm, ps, Cf, io, wbuf, with_ctx, TM)
    return C.finish()


def gla_p1(C, cm, ps, Af, Bf, io, wbuf, TM):
    S = C.S
    HT_in, w_in, wa1, wa2a = io["ht_in"], io["w_in"], io["wa1"], io["wa2a"]
    qT_o, kT_o, kt_o, vt_o, g_o, sr_o = io["qT"], io["kT"], io["ktok"], io["vtok"], io["gsp"], io["srT"]
    wa1b = C.sb("wa1b", [128, 2, KC, 16], BF16)
    wa2s = C.sb("wa2s", [33, 2, 512], F32)
    a1sb = C.sb("a1sb", [33, 2, TM], F32)
    S.dma("pool", "g_w", wbuf[:], w_in.rearrange("(kc p) f -> p kc f", p=128), writes=["wbuf"])
    for d in range(2):
        S.dma("pool", "g_wa1", wa1b[:, d, :, :], wa1[d].rearrange("(kc p) r -> p kc r", p=128), writes=["wa1b"])
        S.dma("sp", "g_wa2", wa2s[:, d, :], wa2a[d], writes=["wa2s"])
    S.memset(a1sb[0:32, :, :], 0.0, writes=["a1sb"])
    S.memset(a1sb[32:33, :, :], 1.0, writes=["a1one"])
    qst = C.sb("qst", [128, GH, TM], BF16)
    kst = C.sb("kst", [128, GH, TM], BF16)
    ktst = C.sb("ktst", [128, TM // 128, 512], BF16)
    vtst = C.sb("vtst", [128, TM // 128, 1024], BF16)
    gst = C.sb("gst", [128, 2, TM // 128, 512], F32)
    srst = C.sb("srst", [128, 8, TM], BF16)
    etmp = [C.sb("etmp%d" % i, [128, 512], F32) for i in range(2)]
    qscale = float(GDK) ** -0.5
    for ti, (t0, T, m) in enumerate(tiles_for(0, True, TM)):
        nb = T // 128
        cm.load_h(HT_in, t0, T)
        cm.prenorm_mod(T, m, Af, Bf, ps, (7, 7))
        ukeys = [("u", kc, 0) for kc in range(KC)]
        for hh in range(2 * GH):
            b = S.nxt("g1a", 3)
            for kc in range(KC):
                S.mm(ps[b][:, :T], wbuf[:, kc, hh * 128:(hh + 1) * 128], cm.u[:, kc, 0:T], kc == 0, kc == KC - 1,
                     reads=["wbuf", ("u", kc, 0)], writes=[("ps", b)])
            if hh < GH:
                S.actf(qst[:, hh, 0:T], ps[b][:, :T], AF.Copy, reads=[("ps", b)], writes=[("qst", hh)], scale=qscale)
            else:
                S.copy(kst[:, hh - GH, 0:T], ps[b][:, :T], reads=[("ps", b)], writes=[("kst", hh - GH)])
        S.dma("sp", "g_st_q", qT_o[:, :, t0:t0 + T], qst[:, :, 0:T], reads=[("qst", h) for h in range(GH)], writes=[("qTo", ti)])
        S.dma("sp", "g_st_k", kT_o[:, :, t0:t0 + T], kst[:, :, 0:T], reads=[("kst", h) for h in range(GH)], writes=[("kTo", ti)])
        for hv in range(8):
            b = S.nxt("g1a", 3)
            for kc in range(KC):
                S.mm(ps[b][:, :T], wbuf[:, kc, 2048 + hv * 128:2048 + (hv + 1) * 128], cm.u[:, kc, 0:T], kc == 0, kc == KC - 1,
                     reads=["wbuf", ("u", kc, 0)], writes=[("ps", b)])
            S.actf(srst[:, hv, 0:T], ps[b][:, :T], AF.Silu, reads=[("ps", b)], writes=[("srst", hv)])
        S.dma("sp", "g_st_sr", sr_o[:, :, t0:t0 + T], srst[:, :, 0:T], reads=[("srst", h) for h in range(8)], writes=[("sro", ti)])
        for blk in range(nb):
            bs = slice(blk * 128, (blk + 1) * 128)
            for part in range(3):
                b = 3 + S.nxt("g1b", 2)
                c0 = 512 + part * 512
                for kc in range(KC):
                    S.mm(ps[b][:, :], cm.u[:, kc, bs], wbuf[:, kc, c0:c0 + 512], kc == 0, kc == KC - 1,
                         reads=["wbuf", ("u", kc, 0)], writes=[("ps", b)])
                if part == 0:
                    S.actf(ktst[:, blk, :], ps[b][:, :], AF.Copy, reads=[("ps", b)], writes=[("ktst", blk)])
                else:
                    S.copy(vtst[:, blk, (part - 1) * 512:part * 512], ps[b][:, :], reads=[("ps", b)], writes=[("vtst", blk, part)])
        S.dma("sp", "g_st_kt", kt_o[t0:t0 + T, :].rearrange("(b p) f -> p b f", p=128), ktst[:, 0:nb, :],
              reads=[("ktst", b_) for b_ in range(nb)], writes=[("kto", ti)])
        S.dma("sp", "g_st_vt", vt_o[t0:t0 + T, :].rearrange("(b p) f -> p b f", p=128), vtst[:, 0:nb, :],
              reads=[("vtst", b_, p_) for b_ in range(nb) for p_ in (1, 2)], writes=[("vto", ti)])
        for d in range(2):
            b = 5 + d
            for kc in range(KC):
                S.mm(ps[b][0:16, :T], wa1b[:, d, kc, :], cm.u[:, kc, 0:T], kc == 0, kc == KC - 1,
                     reads=["wa1b", ("u", kc, 0)], writes=[("ps", b)])
            S.actf(a1sb[0:16, d, 0:T], ps[b][0:16, :T], AF.Copy, reads=[("ps", b)], writes=[("a1", d)])
            for blk in range(nb):
                bb = 3 + S.nxt("g1b", 2)
                S.mm(ps[bb][:, :], a1sb[0:33, d, blk * 128:(blk + 1) * 128], wa2s[0:33, d, :], True, True,
                     reads=[("a1", d), "a1sb", "a1one", "wa2s"], writes=[("ps", bb)])
                e = S.nxt("etmp", 2)
                S.actf(etmp[e][:], ps[bb][:, :], AF.Exp, reads=[("ps", bb)], writes=[("etmp", e)], scale=-1.0)
                S.actf(gst[:, d, blk, :], etmp[e][:], AF.Ln, reads=[("etmp", e)], writes=[("gst", d, blk)], bias=1.0)
            S.dma("sp", "g_st_g", g_o[d, t0:t0 + T, :].rearrange("(b p) f -> p b f", p=128), gst[:, d, 0:nb, :],
                  reads=[("gst", d, b_) for b_ in range(nb)], writes=[("go", d, ti)])


def gla_p2(C, ps, io):
    S = C.S
    qT_i, kT_i, kt_i, vt_i, g_i, o_o = io["qT"], io["kT"], io["ktok"], io["vtok"], io["gsp"], io["oT"]
    cst = io
    LI = [C.sb("LIs%d" % d, [128, 130], F32) for d in range(2)]
    U2 = [C.sb("U2s%d" % d, [128, 128], F32) for d in range(2)]
    M2 = [C.sb("M2s%d" % d, [128, 128], F32) for d in range(2)]
    for d in range(2):
        S.dma("sp", "g_c", LI[d][:], cst["LI%d" % d], writes=[("LI", d)])
        S.dma("sp", "g_c", U2[d][:], cst["U2%d" % d], writes=[("U2", d)])
        S.dma("sp", "g_c", M2[d][:], cst["M2%d" % d], writes=[("M2", d)])
    NSL = 2
    qb_ = [[C.sb("q%d_%d" % (d, s_), [128, GH, 128], BF16) for s_ in range(NSL)] for d in range(2)]
    kb_ = [[C.sb("k%d_%d" % (d, s_), [128, GH, 128], BF16) for s_ in range(NSL)] for d in range(2)]
    ktb = [[C.sb("kt%d_%d" % (d, s_), [128, 512], BF16) for s_ in range(NSL)] for d in range(2)]
    vtb = [[C.sb("vt%d_%d" % (d, s_), [128, 1024], BF16) for s_ in range(NSL)] for d in range(2)]
    gb = [[C.sb("g%d_%d" % (d, s_), [128, 512], F32) for s_ in range(NSL)] for d in range(2)]
    ee = [[C.sb("e%d_%d" % (d, h), [128, 130], F32) for h in range(GH)] for d in range(2)]
    einv = [C.sb("einv%d" % d, [128, 128], F32) for d in range(2)]
    qd = [C.sb("qd%d" % d, [128, GH, 128], BF16) for d in range(2)]
    ki = [C.sb("ki%d" % d, [128, GH, 128], BF16) for d in range(2)]
    er = [C.sb("er%d" % d, [128, 512], F32) for d in range(2)]
    kend = [C.sb("kend%d" % d, [128, 512], BF16) for d in range(2)]
    sT = [C.sb("sT%d" % d, [128, 128], BF16) for d in range(2)]
    st = [[C.sb("st%d_%d" % (d, h), [128, GDV], F32) for h in range(GH)] for d in range(2)]
    stb = [[C.sb("stb%d_%d" % (d, h), [128, GDV], BF16) for h in range(GH)] for d in range(2)]
    oo = [[C.sb("oo%d_%d" % (d, s_), [128, 8, 128], F32) for s_ in range(1)] for d in range(2)]
    for d in range(2):
        for h in range(GH):
            S.memset(st[d][h][:], 0.0, writes=[("st", d, h)])
            S.memset(stb[d][h][:], 0.0, writes=[("stb", d, h)], eng="pool")
    order = [list(range(NBLK)), [1, 0] + list(range(NBLK - 1, 1, -1))]
    for step in range(NBLK):
        for d in range(2):
            blk = order[d][step]
            t0 = blk * 128
            sl = step % NSL
            grp = "g_ld%d_%d" % (d, sl)
            S.dma("sp", grp, qb_[d][sl][:], qT_i[:, :, t0:t0 + 128], writes=[("q", d, sl)])
            S.dma("sp", grp, kb_[d][sl][:], kT_i[:, :, t0:t0 + 128], writes=[("k", d, sl)])
            S.dma("sp", grp, ktb[d][sl][:], kt_i[t0:t0 + 128, :], writes=[("kt", d, sl)])
            S.dma("sp", grp, vtb[d][sl][:], vt_i[t0:t0 + 128, :], writes=[("vt", d, sl)])
            S.dma("sp", grp, gb[d][sl][:], g_i[d, t0:t0 + 128, :], writes=[("g", d, sl)])
            bC, bR, bO0, bO1 = 4 * d, 4 * d + 1, 4 * d + 2, 4 * d + 3
            for h in range(GH):
                S.mm(ps[bC][:, 0:130], gb[d][sl][:, h * 128:(h + 1) * 128], LI[d][:], True, True,
                     reads=[("g", d, sl), ("LI", d)], writes=[("ps", bC)])
                S.actf(ee[d][h][:], ps[bC][:, 0:130], AF.Exp, reads=[("ps", bC)], writes=[("e", d, h)])
                S.actf(einv[d][:], ps[bC][:, 0:128], AF.Exp, reads=[("ps", bC)], writes=[("einv", d)], scale=-1.0)
                S.tt(qd[d][:, h, :], qb_[d][sl][:, h, :], ee[d][h][:, 0:128], ALU.mult,
                     reads=[("q", d, sl), ("e", d, h)], writes=[("qd", d, h)])
                S.tt(ki[d][:, h, :], kb_[d][sl][:, h, :], einv[d][:], ALU.mult,
                     reads=[("k", d, sl), ("einv", d)], writes=[("ki", d, h)])
            S.mm(ps[bR][:, :], U2[d][:], gb[d][sl][:], True, True, reads=[("U2", d), ("g", d, sl)], writes=[("ps", bR)])
            S.actf(er[d][:], ps[bR][:, :], AF.Exp, reads=[("ps", bR)], writes=[("er", d)])
            S.tt(kend[d][:], ktb[d][sl][:], er[d][:], ALU.mult, reads=[("kt", d, sl), ("er", d)], writes=[("kend", d)])
            osl = 0
            for h in range(GH):
                S.mm(ps[bC][:, 0:128], ki[d][:, h, :], qd[d][:, h, :], True, True,
                     reads=[("ki", d, h), ("qd", d, h)], writes=[("ps", bC)])
                S.tt(sT[d][:], ps[bC][:, 0:128], M2[d][:], ALU.mult, reads=[("ps", bC), ("M2", d)], writes=[("sT", d)])
                for c in ((0, 1) if d == 0 else (1, 0)):
                    rs = slice(c * 64, (c + 1) * 64)
                    for vc in range(2):
                        bo = bO0 if vc == 0 else bO1
                        v0 = h * GDV + vc * 128
                        S.mm(ps[bo][:, rs], vtb[d][sl][rs, v0:v0 + 128], sT[d][rs, rs], True, False,
                             reads=[("vt", d, sl), ("sT", d)], writes=[("ps", bo)])
                        S.mm(ps[bo][:, rs], stb[d][h][:, vc * 128:(vc + 1) * 128], qd[d][:, h, rs], False, True,
                             reads=[("stb", d, h), ("qd", d, h)], writes=[("ps", bo)])
                    S.mm(ps[bR][:, 0:GDV], kend[d][rs, h * 128:(h + 1) * 128], vtb[d][sl][rs, h * GDV:(h + 1) * GDV], True, True,
                         reads=[("kend", d), ("vt", d, sl)], writes=[("ps", bR)])
                    S.stt(st[d][h][:], st[d][h][:], ee[d][h][:, 128 + c:129 + c], ps[bR][:, 0:GDV], ALU.mult, ALU.add,
                          reads=[("st", d, h), ("e", d, h), ("ps", bR)], writes=[("st", d, h)])
                    S.actf(stb[d][h][:], st[d][h][:], AF.Copy, reads=[("st", d, h)], writes=[("stb", d, h)])
                S.actf(oo[d][osl][:, 2 * h, :], ps[bO0][:, 0:128], AF.Copy, reads=[("ps", bO0)], writes=[("oo", d, osl, 2 * h)])
                S.copy(oo[d][osl][:, 2 * h + 1, :], ps[bO1][:, 0:128], reads=[("ps", bO1)], writes=[("oo", d, osl, 2 * h + 1)])
            S.dma("sp", "g_ost%d" % d, o_o[d, :, :, t0:t0 + 128], oo[d][osl][:],
                  reads=[("oo", d, osl, hv) for hv in range(8)], writes=[("oT", d, blk)])


def gla_p3(C, cm, ps, Cf, io, wbuf, with_ctx, TM):
    S = C.S
    HT_in, o_i, sr_i, w_o, ogain, HT_out = io["ht_in"], io["oT"], io["srT"], io["w_o"], io["ogain"], io["ht_out"]
    C.out_groups.append("g_hst")
    og = C.sb("og", [128, 8], F32)
    S.dma("sp", "g_og", og[:], ogain, writes=["og"])
    S.dma("pool", "g_w", wbuf[:, :, 0:D], w_o.rearrange("(kc p) f -> p kc f", p=128), writes=["wbuf"])
    of = C.sb("of", [128, 8, TM], F32)
    ob = C.sb("ob", [128, 8, TM], F32)
    sr = C.sb("sr", [128, 8, TM], BF16)
    z = C.sb("z", [128, 8, TM], BF16)
    y = C.sb("y", [128, KC, TM], F32)
    rstdy = C.sb("rstdy", [128, TM], F32)
    ro = [C.sb("ro%d" % i, [128, TM], F32) for i in range(2)]
    if not with_ctx:
        C.out_groups.append("g_cp")
        S.dma("sp", "g_cp0", cm.h[:, :, 0:NCTX], ht_view(HT_in)[:, :, 0:NCTX],
              writes=[("h", kc, s) for kc in range(KC) for s in range(2)])
        S.dma("sp", "g_cp", ht_view(HT_out)[:, :, 0:NCTX], cm.h[:, :, 0:NCTX],
              reads=[("h", kc, s) for kc in range(KC) for s in range(2)], writes=[("HTout", 0)])
    for ti, (t0, T, m) in enumerate(tiles_for(0, with_ctx, TM)):
        S.dma("sp", "g_of", of[:, :, 0:T], o_i[0, :, :, t0:t0 + T], writes=[("of", hv) for hv in range(8)])
        S.dma("sp", "g_ob", ob[:, :, 0:T], o_i[1, :, :, t0:t0 + T], writes=[("ob", hv) for hv in range(8)])
        S.dma("sp", "g_sr", sr[:, :, 0:T], sr_i[:, :, t0:t0 + T], writes=["sr"])
        for h in range(GH):
            b = S.nxt("g3ss", 2)
            for vc in range(2):
                hv = 2 * h + vc
                S.tt(of[:, hv, 0:T], of[:, hv, 0:T], ob[:, hv, 0:T], ALU.add, reads=[("of", hv), ("ob", hv)], writes=[("of", hv)])
                k = S.nxt("sq", 2)
                S.actf(cm.sq[k][:, :T], of[:, hv, 0:T], AF.Square, reads=[("of", hv)], writes=[("sq", k)])
                S.mm(ps[b][:, :T], cm.ones[:], cm.sq[k][:, :T], vc == 0, vc == 1, reads=["ones", ("sq", k)], writes=[("ps", b)])
            ri = S.nxt("ro", 2)
            cm.rms_from_psum_ss(ps[b][:, :T], T, ro[ri][:, 0:T], (("ps", b), ("ro", ri)), GDV)
            for vc in range(2):
                hv = 2 * h + vc
                i = S.nxt("tmp", 2)
                S.tt(cm.tmp[i][:, :T], of[:, hv, 0:T], ro[ri][:, 0:T], ALU.mult, reads=[("of", hv), ("ro", ri)], writes=[("tmp", i)])
                S.stt(z[:, hv, 0:T], cm.tmp[i][:, :T], og[:, hv:hv + 1], sr[:, hv, 0:T], ALU.mult, ALU.mult,
                      reads=[("tmp", i), "og", "sr"], writes=[("z", hv)])
        for dc in range(KC):
            b = 2 + S.nxt("g3y", 4)
            for hv in range(8):
                S.mm(ps[b][:, :T], wbuf[:, hv, dc * 128:(dc + 1) * 128], z[:, hv, 0:T], hv == 0, hv == 7,
                     reads=["wbuf", ("z", hv)], writes=[("ps", b)])
            S.actf(y[:, dc, 0:T], ps[b][:, :T], AF.Copy, reads=[("ps", b)], writes=[("y", dc, 0)])
            k = S.nxt("sq", 2)
            S.actf(cm.sq[k][:, :T], ps[b][:, :T], AF.Square, reads=[("ps", b)], writes=[("sq", k)])
            S.mm(ps[6][:, :T], cm.ones[:], cm.sq[k][:, :T], dc == 0, dc == KC - 1, reads=["ones", ("sq", k)], writes=[("ps", 6)])
        cm.rms_from_psum_ss(ps[6][:, :T], T, rstdy[:, 0:T], (("ps", 6), ("rstdy", 0)), D)
        cm.load_h(HT_in, t0, T)
        cm.postnorm_residual_store(y, rstdy, T, m, Cf, HT_out, t0)

def prog_prep():
    C = Ctx()
    S = C.S
    x = C.din("x", [NX, D])
    cx = C.din("ctx", [NCTX, D])
    cin = C.din("cin", [128, KC, 9])
    ada_w = C.din("ada_w", [4, D, 1152])
    ada_b = C.din("ada_b", [4, 128, 9])
    ident = C.din("ident", [128, 128])
    HT = C.dout("ht_out", [D, NTOK])
    modT = C.dout("modp", [4, 128, 9, 9])
    C.out_groups += ["g_hst", "g_modst"]
    ps = C.psum_banks()
    idsb = C.sb("idsb", [128, 128], F32)
    S.dma("sp", "g_id", idsb[:], ident, writes=["ident"])
    xin = [C.sb("xin%d" % i, [128, 4, D], F32) for i in range(2)]
    hT = [C.sb("hT%d" % i, [128, KC, 512], F32) for i in range(2)]
    groups = [(cx, 0, 2, 0)] + [(x, g * 512, 4, NCTX + g * 512) for g in range(8)]
    for gi, (src, r0, nb, t0) in enumerate(groups):
        xi = S.nxt("xin", 2)
        S.dma("sp", "g_xin%d" % xi, xin[xi][:, 0:nb, :], src[r0:r0 + nb * 128, :].rearrange("(b p) d -> p b d", p=128),
              writes=[("xin", xi)])
        hi = S.nxt("hT", 2)
        for kc in range(KC):
            for b in range(nb):
                S.tr(ps[kc][:, b * 128:(b + 1) * 128], xin[xi][:, b, kc * 128:(kc + 1) * 128], idsb[:],
                     reads=[("xin", xi), "ident"], writes=[("ps", kc)])
            if kc % 2 == 0:
                S.actf(hT[hi][:, kc, 0:nb * 128], ps[kc][:, 0:nb * 128], AF.Copy, reads=[("ps", kc)], writes=[("hT", hi, kc)])
            else:
                S.copy(hT[hi][:, kc, 0:nb * 128], ps[kc][:, 0:nb * 128], reads=[("ps", kc)], writes=[("hT", hi, kc)])
        S.dma("sp", "g_hst", ht_view(HT)[:, :, t0:t0 + nb * 128], hT[hi][:, :, 0:nb * 128],
              reads=[("hT", hi, kc) for kc in range(KC)], writes=[("HT", gi)])
    csb = C.sb("csb", [128, KC, 9], F32)
    sc = C.sb("sc", [128, KC, 9], F32)
    S.dma("sp", "g_cin", csb[:], cin, writes=["csb"])
    S.actf(sc[:], csb[:], AF.Silu, reads=["csb"], writes=["sc"])
    aw = [C.sb("aw%d" % i, [128, KC, 1152], F32) for i in range(2)]
    adab = [C.sb("adab%d" % i, [128, 9], F32) for i in range(2)]
    modsb = [C.sb("modo%d" % i, [128, 9, 9], F32) for i in range(2)]
    for l in range(4):
        pb = l % 2
        S.dma("sp", "g_adab%d" % pb, adab[pb][:], ada_b[l], writes=[("adab", pb)])
        S.dma("sp", "g_aw%d" % pb, aw[pb][:], ada_w[l].rearrange("(kc p) f -> p kc f", p=128), writes=[("aw", pb)])
        for f in range(9):
            for kc in range(KC):
                S.mm(ps[pb][:, 9 * f:9 * f + 9], aw[pb][:, kc, f * 128:(f + 1) * 128], sc[:, kc, :], kc == 0, kc == KC - 1,
                     reads=[("aw", pb), "sc"], writes=[("psm", pb)])
        for f in range(9):
            S.ts(modsb[pb][:, f, :], ps[pb][:, 9 * f:9 * f + 9], adab[pb][:, f:f + 1], None, ALU.add, ALU.bypass,
                 reads=[("psm", pb), ("adab", pb)], writes=[("modo", pb, f)])
        S.dma("sp", "g_modst", modT[l], modsb[pb][:], reads=[("modo", pb, f) for f in range(9)], writes=[("modT", l)])
    return C.finish()


def prog_final():
    C = Ctx()
    S = C.S
    HT = C.din("ht_in", [D, NTOK])
    ident = C.din("ident", [128, 128])
    out = C.dout("out", [NX, D])
    C.out_groups += ["g_ost"]
    ps = C.psum_banks()
    idsb = C.sb("idsb", [128, 128], F32)
    S.dma("sp", "g_id", idsb[:], ident, writes=["ident"])
    hT = [C.sb("hT%d" % i, [128, KC, 512], F32) for i in range(2)]
    xo = [C.sb("xo%d" % i, [128, 4, D], F32) for i in range(2)]
    for g in range(8):
        t0 = NCTX + g * 512
        hi = S.nxt("hT", 2)
        S.dma("sp", "g_hin%d" % hi, hT[hi][:], ht_view(HT)[:, :, t0:t0 + 512], writes=[("hT", hi)])
        xi = S.nxt("xo", 2)
        for b in range(4):
            for half in range(2):
                bank = (b % 4) * 2 + half
                for k4 in range(4):
                    kc = half * 4 + k4
                    S.tr(ps[bank][:, k4 * 128:(k4 + 1) * 128], hT[hi][:, kc, b * 128:(b + 1) * 128], idsb[:],
                         reads=[("hT", hi), "ident"], writes=[("ps", bank)])
                if half == 0:
                    S.actf(xo[xi][:, b, 0:512], ps[bank][:], AF.Copy, reads=[("ps", bank)], writes=[("xo", xi, b, half)])
                else:
                    S.copy(xo[xi][:, b, 512:1024], ps[bank][:], reads=[("ps", bank)], writes=[("xo", xi, b, half)])
        S.dma("sp", "g_ost", out[g * 512:(g + 1) * 512, :].rearrange("(b p) d -> p b d", p=128), xo[xi][:],
              reads=[("xo", xi, b, h2) for b in range(4) for h2 in range(2)], writes=[("out", g)])
    return C.finish()


_PROGS = {}


def _prog(key, fn, *a):
    if key not in _PROGS:
        _PROGS[key] = fn(*a)
    return _PROGS[key]


def _launch(nc, in_maps):
    res = run_bass_kernel_spmd(nc, in_maps, core_ids=list(range(NCORES)))
    return res.results


def _col(v):
    return np.ascontiguousarray(np.asarray(v, np.float32).reshape(KC, 128).T)


def kernel(x, c, ctx, c_ctx, ada_w, ada_b, norm_pre, norm_post, ffn_w_in, ffn_w_out,
           attn_w_qkv, attn_q_gain, attn_k_gain, attn_w_o,
           gla_w_in, gla_wa1, gla_wa2, gla_ba, gla_o_gain, gla_w_o, _stop_after=None, _state=None):
    f32 = lambda a: np.ascontiguousarray(np.asarray(a, dtype=np.float32))
    x, c, ctx, c_ctx = f32(x), f32(c), f32(ctx), f32(c_ctx)
    ada_w, ada_b = f32(ada_w), f32(ada_b)
    norm_pre, norm_post = f32(norm_pre), f32(norm_post)
    ffn_w_in, ffn_w_out = f32(ffn_w_in), f32(ffn_w_out)
    ident = np.eye(128, dtype=np.float32)
    cin = np.ascontiguousarray(np.stack([_col(c[b]) for b in range(NCORES)] + [_col(c_ctx)], axis=-1))
    in_maps = []
    for k in range(NCORES):
        awk = np.ascontiguousarray(ada_w[:, :, k * 1152:(k + 1) * 1152])
        abk = np.ascontiguousarray(ada_b[:, k * 1152:(k + 1) * 1152].reshape(4, 9, 128).transpose(0, 2, 1))
        in_maps.append({"x": x[k], "ctx": ctx[k], "cin": cin, "ada_w": awk, "ada_b": abk, "ident": ident})
    if _state is None:
        r = _launch(_prog("prep", prog_prep), in_maps)
        HT = [r[b]["ht_out"] for b in range(NCORES)]
        full = np.concatenate([r[k]["modp"] for k in range(NCORES)], axis=2)
        modT = [np.ascontiguousarray(np.stack([full[..., b], full[..., 8]], axis=2)) for b in range(NCORES)]
        k0 = 0
    else:
        HT, modT, k0 = _state

    def ffn(i, j, with_ctx):
        nonlocal HT
        nc = _prog(("ffn", j, with_ctx), prog_ffn, j, with_ctx)
        maps = [{"ht_in": HT[b], "w_in": ffn_w_in[i, j // 2], "w_out": ffn_w_out[i, j // 2],
                 "modT": np.ascontiguousarray(modT[b][i]), "gpre": _col(norm_pre[i, j]), "gpost": _col(norm_post[i, j])}
                for b in range(NCORES)]
        rr = _launch(nc, maps)
        HT = [rr[b]["ht_out"] for b in range(NCORES)]

    cosT, sinT, RTm = rope_consts()

    def attn(i):
        nonlocal HT
        mi = i // 2
        nc = _prog("attn", prog_attn)
        qkg = np.ascontiguousarray(np.stack([f32(attn_q_gain)[mi], f32(attn_k_gain)[mi]], axis=-1))
        maps = [{"ht_in": HT[b], "w_qkv": f32(attn_w_qkv)[mi], "w_o": f32(attn_w_o)[mi],
                 "modT": np.ascontiguousarray(modT[b][i]), "gpre": _col(norm_pre[i, 1]), "gpost": _col(norm_post[i, 1]),
                 "qkg": qkg, "cosT": cosT, "sinT": sinT, "RT": RTm} for b in range(NCORES)]
        rr = _launch(nc, maps)
        HT = [rr[b]["ht_out"] for b in range(NCORES)]

    gcst = gla_consts()

    def gla(i):
        nonlocal HT
        mi = i // 2
        last = i == 3
        wa2a = np.zeros((2, 33, 512), np.float32)
        wa2a[:, 0:16, :] = f32(gla_wa2)[mi]
        wa2a[:, 32, :] = f32(gla_ba)[mi]
        maps = [dict(gcst, ht_in=HT[b], w_in=f32(gla_w_in)[mi], wa1=f32(gla_wa1)[mi], wa2a=wa2a,
                     w_o=f32(gla_w_o)[mi], ogain=_col(f32(gla_o_gain)[mi]),
                     modT=np.ascontiguousarray(modT[b][i]), gpre=_col(norm_pre[i, 1]), gpost=_col(norm_post[i, 1]))
                for b in range(NCORES)]
        r3 = _launch(_prog(("gla", not last), prog_gla, not last), maps)
        HT = [r3[b]["ht_out"] for b in range(NCORES)]

    plan = []
    for i in range(4):
        last = i == 3
        plan += [("ffn", i, 0, True), ("mix", i), ("ffn", i, 2, not last)]
    for k, st in enumerate(plan):
        if k < k0:
            continue
        if _stop_after is not None and k >= _stop_after:
            break
        if st[0] == "ffn":
            ffn(st[1], st[2], st[3])
        elif st[1] % 2 == 0:
            attn(st[1])
        else:
            gla(st[1])
        if _stop_after is not None:
            import pickle
            pickle.dump((HT, modT), open("_state_%d.pkl" % (k + 1), "wb"))
    if _stop_after is not None:
        return HT, modT
    rr = _launch(_prog("final", prog_final), [{"ht_in": HT[b], "ident": ident} for b in range(NCORES)])
    return np.stack([rr[b]["out"] for b in range(NCORES)], axis=0)
```

```python
import numpy as np
from contextlib import ExitStack
import concourse.bass as bass
import concourse.mybir as mybir
from concourse.bass_utils import run_bass_kernel_spmd

F32 = mybir.dt.float32
BF16 = mybir.dt.bfloat16
AF = mybir.ActivationFunctionType
ALU = mybir.AluOpType

D = 1024
KC = 8
NCTX = 256
NX = 4096
NTOK = NCTX + NX
DFF = 2816
FC = DFF // 128
EPS = 1e-6
NCORES = 8
HD = 128
NH = 8
NKV = 2


class Op:
    __slots__ = ("eng", "fn", "waits", "signal", "count", "group", "is_dma")

    def __init__(self, eng, fn, group=None):
        self.eng = eng
        self.fn = fn
        self.waits = []
        self.signal = False
        self.count = None
        self.group = group
        self.is_dma = group is not None


class Sched:
    ENGS = ("pe", "act", "dve", "pool", "sp")

    def __init__(self, nc):
        self.nc = nc
        self.ops = {e: [] for e in self.ENGS}
        self.res = {}
        self.groups = {}
        self.rot = {}

    def nxt(self, name, n):
        i = self.rot.get(name, 0)
        self.rot[name] = i + 1
        return i % n

    def op(self, eng, fn, reads=(), writes=(), group=None):
        o = Op(eng, fn, group)
        deps = {}
        for r in reads:
            ent = self.res.get(r)
            if ent is not None and ent[0] is not None:
                deps[id(ent[0])] = (ent[0], True)
        for r in writes:
            ent = self.res.get(r)
            if ent is not None:
                if ent[0] is not None:
                    deps[id(ent[0])] = (ent[0], True)
                for rd in ent[1]:
                    if id(rd) not in deps:
                        deps[id(rd)] = (rd, False)
        for d, hard in deps.values():
            if d is o:
                continue
            if (not d.is_dma) and (not o.is_dma) and d.eng == eng:
                if eng == "pe" or not hard:
                    continue
            d.signal = True
            o.waits.append(d)
        for r in reads:
            ent = self.res.setdefault(r, [None, []])
            ent[1].append(o)
        for r in writes:
            self.res[r] = [o, []]
        if o.is_dma:
            o.signal = True
            self.groups.setdefault(group, None)
        self.ops[eng].append(o)
        return o

    def mm(self, out, lhsT, rhs, start, stop, reads, writes):
        return self.op("pe", lambda e: e.matmul(out, lhsT, rhs, start=start, stop=stop), reads, writes)

    def tr(self, out, in_, ident, reads, writes):
        return self.op("pe", lambda e: e.transpose(out, in_, ident), reads, writes)

    def actf(self, out, in_, func, reads, writes, bias=0.0, scale=1.0):
        return self.op("act", lambda e: e.activation(out, in_, func, bias=bias, scale=scale), reads, writes)

    def tt(self, out, a, b, op, reads, writes, eng="dve"):
        return self.op(eng, lambda e: e.tensor_tensor(out, a, b, op), reads, writes)

    def ts(self, out, a, s1, s2, op0, op1, reads, writes, eng="dve"):
        return self.op(eng, lambda e: e.tensor_scalar(out, a, s1, s2, op0, op1), reads, writes)

    def stt(self, out, in0, scalar, in1, op0, op1, reads, writes, eng="dve"):
        return self.op(eng, lambda e: e.scalar_tensor_tensor(out, in0, scalar, in1, op0, op1), reads, writes)

    def recip(self, out, in_, reads, writes):
        return self.op("dve", lambda e: e.reciprocal(out, in_), reads, writes)

    def copy(self, out, in_, reads, writes, eng="dve"):
        return self.op(eng, lambda e: e.tensor_copy(out, in_), reads, writes)

    def memset(self, ap, val, writes, eng="dve"):
        return self.op(eng, lambda e: e.memset(ap, val), (), writes)

    def dma(self, queue, group, out, in_, reads=(), writes=()):
        return self.op(queue, lambda e: e.dma_start(out=out, in_=in_), reads, writes, group=group)

    def barrier(self):
        lasts = []
        for e in self.ENGS:
            lc = None
            for o in self.ops[e]:
                if not o.is_dma and o.fn is not None:
                    lc = o
            if lc is not None:
                lasts.append(lc)
        lastg = {}
        for e in self.ENGS:
            for o in self.ops[e]:
                if o.is_dma:
                    lastg[o.group] = o
        lasts += list(lastg.values())
        for e in self.ENGS:
            o = Op(e, None)
            for d in lasts:
                if (not d.is_dma) and d.eng == e:
                    continue
                d.signal = True
                o.waits.append(d)
            self.ops[e].append(o)
        self.res = {}

    def emit(self, final_groups=()):
        nc = self.nc
        with ExitStack() as es:
            esem = {e: es.enter_context(nc.semaphore("s_" + e)) for e in self.ENGS}
            gsem = {g: es.enter_context(nc.semaphore("g_" + str(g))) for g in self.groups}
            gcount = {g: 0 for g in self.groups}
            gq = {}
            for e in self.ENGS:
                c = 0
                for o in self.ops[e]:
                    if o.is_dma:
                        assert gq.setdefault(o.group, e) == e, ("group on two queues", o.group)
                        gcount[o.group] += 16
                        o.count = gcount[o.group]
                    elif o.signal:
                        c += 1
                        o.count = c
            block = es.enter_context(nc.Block())

            def run(e, eng):
                known = {}
                for o in self.ops[e]:
                    for d in o.waits:
                        if d.is_dma:
                            sem, key = gsem[d.group], ("g", d.group)
                        else:
                            sem, key = esem[d.eng], ("e", d.eng)
                        if known.get(key, 0) >= d.count:
                            continue
                        known[key] = d.count
                        eng.wait_ge(sem, d.count)
                    if o.fn is None:
                        continue
                    ins = o.fn(eng)
                    if o.is_dma:
                        ins.then_inc(gsem[o.group], 16)
                    elif o.signal:
                        ins.then_inc(esem[e], 1)
                if e == "sp":
                    for g in final_groups:
                        if gcount.get(g, 0) > 0:
                            eng.wait_ge(gsem[g], gcount[g])

            @block.tensor
            def _(eng):
                run("pe", eng)

            @block.scalar
            def _(eng):
                run("act", eng)

            @block.vector
            def _(eng):
                run("dve", eng)

            @block.gpsimd
            def _(eng):
                run("pool", eng)

            @block.sync
            def _(eng):
                run("sp", eng)


class Ctx:
    def __init__(self):
        self.nc = bass.Bass("TRN2", target_bir_lowering=False)
        self.es = ExitStack()
        self.S = Sched(self.nc)
        self.out_groups = []

    def din(self, name, shape, dt=F32):
        return self.nc.dram_tensor(name, list(shape), dt, kind="ExternalInput").ap()

    def dout(self, name, shape, dt=F32):
        return self.nc.dram_tensor(name, list(shape), dt, kind="ExternalOutput").ap()

    def dscr(self, name, shape, dt=F32):
        return self.nc.dram_tensor(name, list(shape), dt).ap()

    def sb(self, name, shape, dt):
        return self.es.enter_context(self.nc.sbuf_tensor(name, list(shape), dt))

    def psum_banks(self):
        return [self.es.enter_context(self.nc.psum_tensor("ps%d" % i, [128, 512], F32)) for i in range(8)]

    def finish(self):
        self.S.emit(final_groups=self.out_groups)
        self.es.close()
        return self.nc


def tiles_for(ntok_from, with_ctx, tmax):
    tl = []
    if with_ctx:
        tl.append((0, NCTX, 1))
    t = NCTX
    while t < NTOK:
        tl.append((t, tmax, 0))
        t += tmax
    return tl


def ht_view(ap):
    return ap.rearrange("(kc p) t -> p kc t", p=128)


def load_consts_mod(C, modT, gpre, gpost, j, coef):
    S = C.S
    modsb = C.sb("modsb", [128, 2, 72], F32)
    gp = C.sb("gp", [128, 8], F32)
    gq = C.sb("gq", [128, 8], F32)
    A = C.sb("modA", [128, 2, 8], F32)
    Cg = C.sb("modC", [128, 2, 8], F32)
    S.dma("sp", "g_mod", modsb[:], modT, writes=["modsb"])
    S.dma("sp", "g_gp", gp[:], gpre, writes=["gp"])
    S.dma("sp", "g_gq", gq[:], gpost, writes=["gq"])
    for m in range(2):
        sc = modsb[:, m, (3 * j + 1) * 8:(3 * j + 2) * 8]
        gt = modsb[:, m, (3 * j + 2) * 8:(3 * j + 3) * 8]
        S.stt(A[:, m, :], sc, 1.0, gp[:], ALU.add, ALU.mult, reads=["modsb", "gp"], writes=[("A", m)])
        S.stt(Cg[:, m, :], gt, coef, gq[:], ALU.mult, ALU.mult, reads=["modsb", "gq"], writes=[("C", m)])
    B = lambda m, kc: modsb[:, m, 3 * j * 8 + kc:3 * j * 8 + kc + 1]
    Af = lambda m, kc: A[:, m, kc:kc + 1]
    Cf = lambda m, kc: Cg[:, m, kc:kc + 1]
    return Af, B, Cf


class Common:
    def __init__(self, C, TM):
        self.C = C
        self.TM = TM
        self.h = C.sb("h", [128, KC, TM], F32)
        self.u = C.sb("u", [128, KC, TM], BF16)
        self.sq = [C.sb("sq%d" % i, [128, 512], BF16) for i in range(2)]
        self.tmp = [C.sb("tmp%d" % i, [128, 512], F32) for i in range(2)]
        self.rstd = C.sb("rstd", [128, TM], F32)
        self.ones = C.sb("ones", [128, 128], BF16)
        C.S.memset(self.ones[:], 1.0, writes=["ones"])

    def load_h(self, HT_in, t0, T):
        S = self.C.S
        S.dma("sp", "g_h", self.h[:, :, 0:T], ht_view(HT_in)[:, :, t0:t0 + T],
              writes=[("h", kc, s) for kc in range(KC) for s in range(2)])

    def rms_from_psum_ss(self, ps_ss, W, dst, key, ndim):
        S = self.C.S
        S.actf(dst, ps_ss, AF.Sqrt, reads=[key[0]], writes=[key[1]], bias=EPS, scale=1.0 / ndim)
        S.recip(dst, dst, reads=[key[1]], writes=[key[1]])

    def prenorm_mod(self, T, m, Af, Bf, ps, bank_ss):
        S = self.C.S
        W = min(T, 512)
        NS = T // W
        for s in range(NS):
            cs = slice(s * W, (s + 1) * W)
            b = bank_ss[s]
            for kc in range(KC):
                i = S.nxt("sq", 2)
                S.actf(self.sq[i][:, :W], self.h[:, kc, cs], AF.Square, reads=[("h", kc, s)], writes=[("sq", i)])
                S.mm(ps[b][:, :W], self.ones[:], self.sq[i][:, :W], kc == 0, kc == KC - 1,
                     reads=["ones", ("sq", i)], writes=[("ps", b)])
            self.rms_from_psum_ss(ps[b][:, :W], W, self.rstd[:, cs], (("ps", b), ("rstd", s)), D)
        for s in range(NS):
            cs = slice(s * W, (s + 1) * W)
            for kc in range(KC):
                i = S.nxt("tmp", 2)
                S.tt(self.tmp[i][:, :W], self.h[:, kc, cs], self.rstd[:, cs], ALU.mult,
                     reads=[("h", kc, s), ("rstd", s)], writes=[("tmp", i)])
                S.actf(self.u[:, kc, cs], self.tmp[i][:, :W], AF.Identity, reads=[("tmp", i), ("A", m), "modsb"],
                       writes=[("u", kc, s)], bias=Bf(m, kc), scale=Af(m, kc))

    def postnorm_residual_store(self, y, rstd_y, T, m, Cf, HT_out, t0):
        S = self.C.S
        W = min(T, 512)
        NS = T // W
        for s in range(NS):
            cs = slice(s * W, (s + 1) * W)
            for kc in range(KC):
                i = S.nxt("tmp", 2)
                S.tt(self.tmp[i][:, :W], y[:, kc, cs], rstd_y[:, cs], ALU.mult,
                     reads=[("y", kc, s), ("rstdy", s)], writes=[("tmp", i)])
                S.stt(self.h[:, kc, cs], self.tmp[i][:, :W], Cf(m, kc), self.h[:, kc, cs], ALU.mult, ALU.add,
                      reads=[("tmp", i), ("C", m), ("h", kc, s)], writes=[("h", kc, s)])
        S.dma("sp", "g_hst", ht_view(HT_out)[:, :, t0:t0 + T], self.h[:, :, 0:T],
              reads=[("h", kc, s) for kc in range(KC) for s in range(2)], writes=[("HTout", t0)])


def prog_ffn(j, with_ctx):
    C = Ctx()
    S = C.S
    TM = 1024
    HT_in = C.din("ht_in", [D, NTOK])
    w_in = C.din("w_in", [D, 2 * DFF])
    w_out = C.din("w_out", [DFF, D])
    modT = C.din("modT", [128, 2, 72])
    gpre = C.din("gpre", [128, 8])
    gpost = C.din("gpost", [128, 8])
    HT_out = C.dout("ht_out", [D, NTOK])
    C.out_groups.append("g_hst")
    if not with_ctx:
        C.out_groups.append("g_cp")
    ps = C.psum_banks()
    Af, Bf, Cf = load_consts_mod(C, modT, gpre, gpost, j, 0.5)
    cm = Common(C, TM)
    actb = C.sb("actb", [128, FC, TM], BF16)
    y = C.sb("y", [128, KC, TM], F32)
    rstdy = C.sb("rstdy", [128, TM], F32)
    win = [C.sb("win%d" % i, [128, KC, 512], BF16) for i in range(3)]
    wo = [C.sb("wo%d" % i, [128, FC, 256], BF16) for i in range(2)]
    sg = [C.sb("sg%d" % i, [128, 512], BF16) for i in range(2)]
    w_in_v = w_in.rearrange("(kc p) f -> p kc f", p=128)
    w_out_v = w_out.rearrange("(fc p) d -> p fc d", p=128)

    if not with_ctx:
        S.dma("sp", "g_cp0", cm.h[:, :, 0:NCTX], ht_view(HT_in)[:, :, 0:NCTX],
              writes=[("h", kc, s) for kc in range(KC) for s in range(2)])
        S.dma("sp", "g_cp", ht_view(HT_out)[:, :, 0:NCTX], cm.h[:, :, 0:NCTX],
              reads=[("h", kc, s) for kc in range(KC) for s in range(2)], writes=[("HTout", 0)])

    for (t0, T, m) in tiles_for(0, with_ctx, TM):
        W = min(T, 512)
        NS = T // W
        cm.load_h(HT_in, t0, T)
        cm.prenorm_mod(T, m, Af, Bf, ps, (6, 7))
        for g in range(11):
            sl = S.nxt("win", 3)
            S.dma("pool", "g_win%d" % sl, win[sl][:, :, 0:256], w_in_v[:, :, g * 256:(g + 1) * 256], writes=[("win", sl)])
            S.dma("pool", "g_win%d" % sl, win[sl][:, :, 256:512], w_in_v[:, :, DFF + g * 256:DFF + (g + 1) * 256],
                  writes=[("win", sl)])
            for f2 in range(2):
                f = g * 2 + f2
                for s in range(NS):
                    cs = slice(s * W, (s + 1) * W)
                    pr = S.nxt("gu", 3)
                    bg, bu = 2 * pr, 2 * pr + 1
                    for kc in range(KC):
                        S.mm(ps[bg][:, :W], win[sl][:, kc, f2 * 128:(f2 + 1) * 128], cm.u[:, kc, cs], kc == 0, kc == KC - 1,
                             reads=[("win", sl), ("u", kc, s)], writes=[("ps", bg)])
                    for kc in range(KC):
                        S.mm(ps[bu][:, :W], win[sl][:, kc, 256 + f2 * 128:256 + (f2 + 1) * 128], cm.u[:, kc, cs], kc == 0,
                             kc == KC - 1, reads=[("win", sl), ("u", kc, s)], writes=[("ps", bu)])
                    i = S.nxt("sg", 2)
                    S.actf(sg[i][:, :W], ps[bg][:, :W], AF.Silu, reads=[("ps", bg)], writes=[("sg", i)])
                    S.tt(actb[:, f, cs], sg[i][:, :W], ps[bu][:, :W], ALU.mult, reads=[("sg", i), ("ps", bu)],
                         writes=[("actb", f, s)])
        for q in range(4):
            sl = S.nxt("wo", 2)
            S.dma("pool", "g_wo%d" % sl, wo[sl][:], w_out_v[:, :, q * 256:(q + 1) * 256], writes=[("wo", sl)])
            for d2 in range(2):
                dc = q * 2 + d2
                for s in range(NS):
                    cs = slice(s * W, (s + 1) * W)
                    b = S.nxt("yb", 6)
                    for f in range(FC):
                        S.mm(ps[b][:, :W], wo[sl][:, f, d2 * 128:(d2 + 1) * 128], actb[:, f, cs], f == 0, f == FC - 1,
                             reads=[("wo", sl), ("actb", f, s)], writes=[("ps", b)])
                    S.actf(y[:, dc, cs], ps[b][:, :W], AF.Copy, reads=[("ps", b)], writes=[("y", dc, s)])
                    i = S.nxt("sq", 2)
                    S.actf(cm.sq[i][:, :W], ps[b][:, :W], AF.Square, reads=[("ps", b)], writes=[("sq", i)])
                    S.mm(ps[6 + s][:, :W], cm.ones[:], cm.sq[i][:, :W], dc == 0, dc == KC - 1,
                         reads=["ones", ("sq", i)], writes=[("ps", 6 + s)])
        for s in range(NS):
            cs = slice(s * W, (s + 1) * W)
            cm.rms_from_psum_ss(ps[6 + s][:, :W], W, rstdy[:, cs], (("ps", 6 + s), ("rstdy", s)), D)
        cm.postnorm_residual_store(y, rstdy, T, m, Cf, HT_out, t0)
    return C.finish()


def prog_attn():
    C = Ctx()
    S = C.S
    TM = 512
    HT_in = C.din("ht_in", [D, NTOK])
    w_qkv = C.din("w_qkv", [D, 1536])
    w_o = C.din("w_o", [D, D])
    modT = C.din("modT", [128, 2, 72])
    gpre = C.din("gpre", [128, 8])
    gpost = C.din("gpost", [128, 8])
    qkg = C.din("qkg", [128, 2])
    cosT = C.din("cosT", [128, NTOK])
    sinT = C.din("sinT", [128, NTOK])
    RTd = C.din("RT", [128, 128])
    HT_out = C.dout("ht_out", [D, NTOK])
    C.out_groups.append("g_hst")
    ps = C.psum_banks()
    Af, Bf, Cf = load_consts_mod(C, modT, gpre, gpost, 1, 1.0)
    cm = Common(C, TM)
    QT = C.sb("QT", [128, NH, NTOK], BF16)
    KT = C.sb("KT", [128, NKV, NTOK], BF16)
    V = C.sb("V", [128, NTOK // 128, 256], BF16)
    wbuf = C.sb("wbuf", [128, KC, 1536], BF16)
    RT = C.sb("RTsb", [128, 128], BF16)
    gains = C.sb("gains", [128, 2], F32)
    cs_t = C.sb("cs_t", [128, 2, 512], F32)
    qb = [C.sb("qb%d" % i, [128, 512], BF16) for i in range(2)]
    rq = [C.sb("rq%d" % i, [128, 512], F32) for i in range(2)]
    t1 = [C.sb("t1_%d" % i, [128, 512], F32) for i in range(2)]
    t2 = [C.sb("t2_%d" % i, [128, 512], F32) for i in range(2)]
    PT = [C.sb("PT%d" % i, [128, 512], BF16) for i in range(4)]
    OT = C.sb("OT", [128, NH, 512], BF16)
    rec = rq
    y = C.sb("y", [128, KC, TM], F32)
    rstdy = C.sb("rstdy", [128, TM], F32)
    S.dma("pool", "g_w", wbuf[:], w_qkv.rearrange("(kc p) f -> p kc f", p=128), writes=["wbuf"])
    S.dma("pool", "g_rt", RT[:], RTd, writes=["RT"])
    S.dma("sp", "g_gn", gains[:], qkg, writes=["gains"])
    tiles = tiles_for(0, True, TM)
    for ti, (t0, T, m) in enumerate(tiles):
        cm.load_h(HT_in, t0, T)
        cm.prenorm_mod(T, m, Af, Bf, ps, (7, 7))
        S.dma("sp", "g_cs", cs_t[:, 0, 0:T], cosT[:, t0:t0 + T], writes=["cs_t"])
        S.dma("sp", "g_cs", cs_t[:, 1, 0:T], sinT[:, t0:t0 + T], writes=["cs_t"])
        for hh in range(NH + NKV):
            b = S.nxt("a1raw", 3)
            for kc in range(KC):
                S.mm(ps[b][:, :T], wbuf[:, kc, hh * 128:(hh + 1) * 128], cm.u[:, kc, 0:T], kc == 0, kc == KC - 1,
                     reads=["wbuf", ("u", kc, 0)], writes=[("ps", b)])
            gi = 0 if hh < NH else 1
            i = S.nxt("qb", 2)
            S.actf(qb[i][:, :T], ps[b][:, :T], AF.Identity, reads=[("ps", b), "gains"], writes=[("qb", i)],
                   scale=gains[:, gi:gi + 1])
            k = S.nxt("sq", 2)
            S.actf(cm.sq[k][:, :T], ps[b][:, :T], AF.Square, reads=[("ps", b)], writes=[("sq", k)])
            bs = 3 + S.nxt("a1ss", 2)
            br = 5 + S.nxt("a1rot", 2)
            S.mm(ps[bs][:, :T], cm.ones[:], cm.sq[k][:, :T], True, True, reads=["ones", ("sq", k)], writes=[("ps", bs)])
            S.mm(ps[br][:, :T], RT[:], qb[i][:, :T], True, True, reads=["RT", ("qb", i)], writes=[("ps", br)])
            S.actf(rq[i][:, :T], ps[bs][:, :T], AF.Sqrt, reads=[("ps", bs)], writes=[("rq", i)], bias=EPS, scale=1.0 / HD)
            S.recip(rq[i][:, :T], rq[i][:, :T], reads=[("rq", i)], writes=[("rq", i)])
            S.tt(t1[i][:, :T], qb[i][:, :T], cs_t[:, 0, 0:T], ALU.mult, reads=[("qb", i), "cs_t"], writes=[("t1", i)])
            S.tt(t2[i][:, :T], ps[br][:, :T], cs_t[:, 1, 0:T], ALU.mult, reads=[("ps", br), "cs_t"], writes=[("t2", i)])
            S.tt(t1[i][:, :T], t1[i][:, :T], t2[i][:, :T], ALU.add, reads=[("t1", i), ("t2", i)], writes=[("t1", i)])
            if hh < NH:
                dst, key = QT[:, hh, t0:t0 + T], ("QT", hh, ti)
            else:
                dst, key = KT[:, hh - NH, t0:t0 + T], ("KT", hh - NH, ti)
            S.tt(dst, t1[i][:, :T], rq[i][:, :T], ALU.mult, reads=[("t1", i), ("rq", i)], writes=[key])
        for blk in range(T // 128):
            for kc in range(KC):
                S.mm(ps[7][:, 0:256], cm.u[:, kc, blk * 128:(blk + 1) * 128], wbuf[:, kc, 1280:1536], kc == 0, kc == KC - 1,
                     reads=["wbuf", ("u", kc, 0)], writes=[("ps", 7)])
            S.actf(V[:, t0 // 128 + blk, :], ps[7][:, 0:256], AF.Copy, reads=[("ps", 7)], writes=[("V", ti, blk)])
    S.dma("pool", "g_w", wbuf[:, :, 0:D], w_o.rearrange("(kc p) f -> p kc f", p=128), reads=[], writes=["wbuf"])

    def kt_tile(kb):
        return 0 if kb < 2 else 1 + (kb - 2) // 4

    def v_key(kb):
        return ("V", 0, kb) if kb < 2 else ("V", 1 + (kb - 2) // 4, (kb - 2) % 4)

    sm_scale = float(HD) ** -0.5
    LA = 3
    for ti, (t0, T, m) in enumerate(tiles):
        nkb = 2 if m == 1 else NTOK // 128
        items = [(hh, kb) for hh in range(NH) for kb in range(nkb)]
        pend = {}

        def issue_s(n, t0=t0, T=T, ti=ti, items=items, pend=pend):
            hh, kb = items[n]
            g = hh // (NH // NKV)
            bsn = S.nxt("a2s", 4)
            S.mm(ps[bsn][:, :T], KT[:, g, kb * 128:(kb + 1) * 128], QT[:, hh, t0:t0 + T], True, True,
                 reads=[("KT", g, kt_tile(kb)), ("QT", hh, ti)], writes=[("ps", bsn)])
            pi = S.nxt("PT", 4)
            S.actf(PT[pi][:, :T], ps[bsn][:, :T], AF.Exp, reads=[("ps", bsn)], writes=[("PT", pi)], scale=sm_scale)
            pend[n] = pi

        for n in range(min(LA, len(items))):
            issue_s(n)
        for n, (hh, kb) in enumerate(items):
            g = hh // (NH // NKV)
            bo = 4 + 2 * (hh % 2)
            bl = bo + 1
            pi = pend.pop(n)
            S.mm(ps[bo][:, :T], V[:, kb, g * 128:(g + 1) * 128], PT[pi][:, :T], kb == 0, kb == nkb - 1,
                 reads=[v_key(kb), ("PT", pi)], writes=[("ps", bo)])
            S.mm(ps[bl][:, :T], cm.ones[:], PT[pi][:, :T], kb == 0, kb == nkb - 1,
                 reads=["ones", ("PT", pi)], writes=[("ps", bl)])
            if n + LA < len(items):
                issue_s(n + LA)
            if kb == nkb - 1:
                ri = S.nxt("rec", 2)
                S.recip(rec[ri][:, :T], ps[bl][:, :T], reads=[("ps", bl)], writes=[("rq", ri)])
                S.tt(OT[:, hh, 0:T], ps[bo][:, :T], rec[ri][:, :T], ALU.mult, reads=[("ps", bo), ("rq", ri)],
                     writes=[("OT", hh)])
        for dc in range(KC):
            b = S.nxt("a2y", 4)
            for hh in range(NH):
                S.mm(ps[b][:, :T], wbuf[:, hh, dc * 128:(dc + 1) * 128], OT[:, hh, 0:T], hh == 0, hh == NH - 1,
                     reads=["wbuf", ("OT", hh)], writes=[("ps", b)])
            S.actf(y[:, dc, 0:T], ps[b][:, :T], AF.Copy, reads=[("ps", b)], writes=[("y", dc, 0)])
            k = S.nxt("sq", 2)
            S.actf(cm.sq[k][:, :T], ps[b][:, :T], AF.Square, reads=[("ps", b)], writes=[("sq", k)])
            S.mm(ps[4][:, :T], cm.ones[:], cm.sq[k][:, :T], dc == 0, dc == KC - 1,
                 reads=["ones", ("sq", k)], writes=[("ps", 4)])
        cm.rms_from_psum_ss(ps[4][:, :T], T, rstdy[:, 0:T], (("ps", 4), ("rstdy", 0)), D)
        cm.load_h(HT_in, t0, T)
        cm.postnorm_residual_store(y, rstdy, T, m, Cf, HT_out, t0)
    return C.finish()


def rope_consts():
    inv = 10000.0 ** (-np.arange(0, 64, 2, dtype=np.float32) / 64.0)
    t = np.arange(NX)
    row = (t // 64).astype(np.float32)
    col = (t % 64).astype(np.float32)
    cosT = np.ones((128, NTOK), np.float32)
    sinT = np.zeros((128, NTOK), np.float32)
    RT = np.zeros((128, 128), np.float32)
    for p in range(128):
        half, within = p // 64, p % 64
        part, f = within // 32, within % 32
        pos = row if half == 0 else col
        ang = (pos * inv[f]).astype(np.float32)
        cosT[p, NCTX:] = np.cos(ang)
        sinT[p, NCTX:] = np.sin(ang)
        if part == 0:
            RT[p + 32, p] = -1.0
        else:
            RT[p - 32, p] = 1.0
    return cosT, sinT, RT


GH = 4
GDK = 128
GDV = 256
NBLK = NTOK // 128


def gla_consts():
    j = np.arange(128)[:, None]
    i = np.arange(128)[None, :]
    same = (j // 64) == (i // 64)
    sc = np.float32(-1.0 / 16.0)
    ind = ((np.arange(128)[:, None] // 64) == np.arange(2)[None, :]).astype(np.float32)
    out = {}
    for d, (le, gt) in enumerate([(j <= i, j > i), (j >= i, j < i)]):
        L2 = (same & le).astype(np.float32)
        U2 = (same & gt).astype(np.float32)
        out["LI%d" % d] = np.ascontiguousarray(np.concatenate([L2 * sc, ind * sc], axis=1))
        out["U2%d" % d] = np.ascontiguousarray(U2 * sc)
        out["M2%d" % d] = np.ascontiguousarray(L2)
    return out


GTM = 256


def prog_gla(with_ctx):
    C = Ctx()
    S = C.S
    TM = GTM
    io = {}
    io["ht_in"] = C.din("ht_in", [D, NTOK])
    io["w_in"] = C.din("w_in", [D, 3072])
    io["wa1"] = C.din("wa1", [2, D, 16])
    io["wa2a"] = C.din("wa2a", [2, 33, 512])
    io["w_o"] = C.din("w_o", [D, D])
    io["ogain"] = C.din("ogain", [128, 8])
    modT = C.din("modT", [128, 2, 72])
    gpre = C.din("gpre", [128, 8])
    gpost = C.din("gpost", [128, 8])
    for n in ("LI0", "LI1", "U20", "U21", "M20", "M21"):
        io[n] = C.din(n, [128, 130 if n.startswith("LI") else 128])
    io["ht_out"] = C.dout("ht_out", [D, NTOK])
    io["qT"] = C.dscr("qT", [128, GH, NTOK], BF16)
    io["kT"] = C.dscr("kT", [128, GH, NTOK], BF16)
    io["ktok"] = C.dscr("ktok", [NTOK, 512], BF16)
    io["vtok"] = C.dscr("vtok", [NTOK, 1024], BF16)
    io["gsp"] = C.dscr("gsp", [2, NTOK, 512], F32)
    io["srT"] = C.dscr("srT", [128, 8, NTOK], BF16)
    io["oT"] = C.dscr("oT", [2, 128, 8, NTOK], F32)
    ps = C.psum_banks()
    Af, Bf, Cf = load_consts_mod(C, modT, gpre, gpost, 1, 1.0)
    cm = Common(C, TM)
    wbuf = C.sb("wbuf", [128, KC, 3072], BF16)
    gla_p1(C, cm, ps, Af, Bf, io, wbuf, TM)
    S.barrier()
    gla_p2(C, ps, io)
    S.barrier()
    gla_p3(C, cm, ps, Cf, io, wbuf, with_ctx, TM)
    return C.finish()


def gla_p1(C, cm, ps, Af, Bf, io, wbuf, TM):
    S = C.S
    HT_in, w_in, wa1, wa2a = io["ht_in"], io["w_in"], io["wa1"], io["wa2a"]
    qT_o, kT_o, kt_o, vt_o, g_o, sr_o = io["qT"], io["kT"], io["ktok"], io["vtok"], io["gsp"], io["srT"]
    wa1b = C.sb("wa1b", [128, 2, KC, 16], BF16)
    wa2s = C.sb("wa2s", [33, 2, 512], F32)
    a1sb = C.sb("a1sb", [33, 2, TM], F32)
    S.dma("pool", "g_w", wbuf[:], w_in.rearrange("(kc p) f -> p kc f", p=128), writes=["wbuf"])
    for d in range(2):
        S.dma("pool", "g_wa1", wa1b[:, d, :, :], wa1[d].rearrange("(kc p) r -> p kc r", p=128), writes=["wa1b"])
        S.dma("sp", "g_wa2", wa2s[:, d, :], wa2a[d], writes=["wa2s"])
    S.memset(a1sb[0:32, :, :], 0.0, writes=["a1sb"])
    S.memset(a1sb[32:33, :, :], 1.0, writes=["a1one"])
    qst = C.sb("qst", [128, GH, TM], BF16)
    kst = C.sb("kst", [128, GH, TM], BF16)
    ktst = C.sb("ktst", [128, TM // 128, 512], BF16)
    vtst = C.sb("vtst", [128, TM // 128, 1024], BF16)
    gst = C.sb("gst", [128, 2, TM // 128, 512], F32)
    srst = C.sb("srst", [128, 8, TM], BF16)
    etmp = [C.sb("etmp%d" % i, [128, 512], F32) for i in range(2)]
    qscale = float(GDK) ** -0.5
    for ti, (t0, T, m) in enumerate(tiles_for(0, True, TM)):
        nb = T // 128
        cm.load_h(HT_in, t0, T)
        cm.prenorm_mod(T, m, Af, Bf, ps, (7, 7))
        ukeys = [("u", kc, 0) for kc in range(KC)]
        for hh in range(2 * GH):
            b = S.nxt("g1a", 3)
            for kc in range(KC):
                S.mm(ps[b][:, :T], wbuf[:, kc, hh * 128:(hh + 1) * 128], cm.u[:, kc, 0:T], kc == 0, kc == KC - 1,
                     reads=["wbuf", ("u", kc, 0)], writes=[("ps", b)])
            if hh < GH:
                S.actf(qst[:, hh, 0:T], ps[b][:, :T], AF.Copy, reads=[("ps", b)], writes=[("qst", hh)], scale=qscale)
            else:
                S.copy(kst[:, hh - GH, 0:T], ps[b][:, :T], reads=[("ps", b)], writes=[("kst", hh - GH)])
        S.dma("sp", "g_st_q", qT_o[:, :, t0:t0 + T], qst[:, :, 0:T], reads=[("qst", h) for h in range(GH)], writes=[("qTo", ti)])
        S.dma("sp", "g_st_k", kT_o[:, :, t0:t0 + T], kst[:, :, 0:T], reads=[("kst", h) for h in range(GH)], writes=[("kTo", ti)])
        for hv in range(8):
            b = S.nxt("g1a", 3)
            for kc in range(KC):
                S.mm(ps[b][:, :T], wbuf[:, kc, 2048 + hv * 128:2048 + (hv + 1) * 128], cm.u[:, kc, 0:T], kc == 0, kc == KC - 1,
                     reads=["wbuf", ("u", kc, 0)], writes=[("ps", b)])
            S.actf(srst[:, hv, 0:T], ps[b][:, :T], AF.Silu, reads=[("ps", b)], writes=[("srst", hv)])
        S.dma("sp", "g_st_sr", sr_o[:, :, t0:t0 + T], srst[:, :, 0:T], reads=[("srst", h) for h in range(8)], writes=[("sro", ti)])
        for blk in range(nb):
            bs = slice(blk * 128, (blk + 1) * 128)
            for part in range(3):
                b = 3 + S.nxt("g1b", 2)
                c0 = 512 + part * 512
                for kc in range(KC):
                    S.mm(ps[b][:, :], cm.u[:, kc, bs], wbuf[:, kc, c0:c0 + 512], kc == 0, kc == KC - 1,
                         reads=["wbuf", ("u", kc, 0)], writes=[("ps", b)])
                if part == 0:
                    S.actf(ktst[:, blk, :], ps[b][:, :], AF.Copy, reads=[("ps", b)], writes=[("ktst", blk)])
                else:
                    S.copy(vtst[:, blk, (part - 1) * 512:part * 512], ps[b][:, :], reads=[("ps", b)], writes=[("vtst", blk, part)])
        S.dma("sp", "g_st_kt", kt_o[t0:t0 + T, :].rearrange("(b p) f -> p b f", p=128), ktst[:, 0:nb, :],
              reads=[("ktst", b_) for b_ in range(nb)], writes=[("kto", ti)])
        S.dma("sp", "g_st_vt", vt_o[t0:t0 + T, :].rearrange("(b p) f -> p b f", p=128), vtst[:, 0:nb, :],
              reads=[("vtst", b_, p_) for b_ in range(nb) for p_ in (1, 2)], writes=[("vto", ti)])
        for d in range(2):
            b = 5 + d
            for kc in range(KC):
                S.mm(ps[b][0:16, :T], wa1b[:, d, kc, :], cm.u[:, kc, 0:T], kc == 0, kc == KC - 1,
                     reads=["wa1b", ("u", kc, 0)], writes=[("ps", b)])
            S.actf(a1sb[0:16, d, 0:T], ps[b][0:16, :T], AF.Copy, reads=[("ps", b)], writes=[("a1", d)])
            for blk in range(nb):
                bb = 3 + S.nxt("g1b", 2)
                S.mm(ps[bb][:, :], a1sb[0:33, d, blk * 128:(blk + 1) * 128], wa2s[0:33, d, :], True, True,
                     reads=[("a1", d), "a1sb", "a1one", "wa2s"], writes=[("ps", bb)])
                e = S.nxt("etmp", 2)
                S.actf(etmp[e][:], ps[bb][:, :], AF.Exp, reads=[("ps", bb)], writes=[("etmp", e)], scale=-1.0)
                S.actf(gst[:, d, blk, :], etmp[e][:], AF.Ln, reads=[("etmp", e)], writes=[("gst", d, blk)], bias=1.0)
            S.dma("sp", "g_st_g", g_o[d, t0:t0 + T, :].rearrange("(b p) f -> p b f", p=128), gst[:, d, 0:nb, :],
                  reads=[("gst", d, b_) for b_ in range(nb)], writes=[("go", d, ti)])


def gla_p2(C, ps, io):
    S = C.S
    qT_i, kT_i, kt_i, vt_i, g_i, o_o = io["qT"], io["kT"], io["ktok"], io["vtok"], io["gsp"], io["oT"]
    cst = io
    LI = [C.sb("LIs%d" % d, [128, 130], F32) for d in range(2)]
    U2 = [C.sb("U2s%d" % d, [128, 128], F32) for d in range(2)]
    M2 = [C.sb("M2s%d" % d, [128, 128], F32) for d in range(2)]
    for d in range(2):
        S.dma("sp", "g_c", LI[d][:], cst["LI%d" % d], writes=[("LI", d)])
        S.dma("sp", "g_c", U2[d][:], cst["U2%d" % d], writes=[("U2", d)])
        S.dma("sp", "g_c", M2[d][:], cst["M2%d" % d], writes=[("M2", d)])
    NSL = 2
    qb_ = [[C.sb("q%d_%d" % (d, s_), [128, GH, 128], BF16) for s_ in range(NSL)] for d in range(2)]
    kb_ = [[C.sb("k%d_%d" % (d, s_), [128, GH, 128], BF16) for s_ in range(NSL)] for d in range(2)]
    ktb = [[C.sb("kt%d_%d" % (d, s_), [128, 512], BF16) for s_ in range(NSL)] for d in range(2)]
    vtb = [[C.sb("vt%d_%d" % (d, s_), [128, 1024], BF16) for s_ in range(NSL)] for d in range(2)]
    gb = [[C.sb("g%d_%d" % (d, s_), [128, 512], F32) for s_ in range(NSL)] for d in range(2)]
    ee = [[C.sb("e%d_%d" % (d, h), [128, 130], F32) for h in range(GH)] for d in range(2)]
    einv = [C.sb("einv%d" % d, [128, 128], F32) for d in range(2)]
    qd = [C.sb("qd%d" % d, [128, GH, 128], BF16) for d in range(2)]
    ki = [C.sb("ki%d" % d, [128, GH, 128], BF16) for d in range(2)]
    er = [C.sb("er%d" % d, [128, 512], F32) for d in range(2)]
    kend = [C.sb("kend%d" % d, [128, 512], BF16) for d in range(2)]
    sT = [C.sb("sT%d" % d, [128, 128], BF16) for d in range(2)]
    st = [[C.sb("st%d_%d" % (d, h), [128, GDV], F32) for h in range(GH)] for d in range(2)]
    stb = [[C.sb("stb%d_%d" % (d, h), [128, GDV], BF16) for h in range(GH)] for d in range(2)]
    oo = [[C.sb("oo%d_%d" % (d, s_), [128, 8, 128], F32) for s_ in range(1)] for d in range(2)]
    for d in range(2):
        for h in range(GH):
            S.memset(st[d][h][:], 0.0, writes=[("st", d, h)])
            S.memset(stb[d][h][:], 0.0, writes=[("stb", d, h)], eng="pool")
    order = [list(range(NBLK)), [1, 0] + list(range(NBLK - 1, 1, -1))]
    for step in range(NBLK):
        sl = step % NSL
        blks = [order[d][step] for d in range(2)]
        bank = [(4 * d, 4 * d + 1, 4 * d + 2, 4 * d + 3) for d in range(2)]
        for d in range(2):
            blk = blks[d]
            t0 = blk * 128
            grp = "g_ld%d_%d" % (d, sl)
            S.dma("sp", grp, qb_[d][sl][:], qT_i[:, :, t0:t0 + 128], writes=[("q", d, sl)])
            S.dma("sp", grp, kb_[d][sl][:], kT_i[:, :, t0:t0 + 128], writes=[("k", d, sl)])
            S.dma("sp", grp, ktb[d][sl][:], kt_i[t0:t0 + 128, :], writes=[("kt", d, sl)])
            S.dma("sp", grp, vtb[d][sl][:], vt_i[t0:t0 + 128, :], writes=[("vt", d, sl)])
            S.dma("sp", grp, gb[d][sl][:], g_i[d, t0:t0 + 128, :], writes=[("g", d, sl)])
            bC, bR, bO0, bO1 = bank[d]
            for h in range(GH):
                S.mm(ps[bC][:, 0:130], gb[d][sl][:, h * 128:(h + 1) * 128], LI[d][:], True, True,
                     reads=[("g", d, sl), ("LI", d)], writes=[("ps", bC)])
                S.actf(ee[d][h][:], ps[bC][:, 0:130], AF.Exp, reads=[("ps", bC)], writes=[("e", d, h)])
                S.actf(einv[d][:], ps[bC][:, 0:128], AF.Exp, reads=[("ps", bC)], writes=[("einv", d)], scale=-1.0)
                S.tt(qd[d][:, h, :], qb_[d][sl][:, h, :], ee[d][h][:, 0:128], ALU.mult,
                     reads=[("q", d, sl), ("e", d, h)], writes=[("qd", d, h)])
                S.tt(ki[d][:, h, :], kb_[d][sl][:, h, :], einv[d][:], ALU.mult,
                     reads=[("k", d, sl), ("einv", d)], writes=[("ki", d, h)])
            S.mm(ps[bR][:, :], U2[d][:], gb[d][sl][:], True, True, reads=[("U2", d), ("g", d, sl)], writes=[("ps", bR)])
            S.actf(er[d][:], ps[bR][:, :], AF.Exp, reads=[("ps", bR)], writes=[("er", d)])
            S.tt(kend[d][:], ktb[d][sl][:], er[d][:], ALU.mult, reads=[("kt", d, sl), ("er", d)], writes=[("kend", d)])
        osl = 0
        for h in range(GH):
            for d in range(2):
                bC, bR, bO0, bO1 = bank[d]
                S.mm(ps[bC][:, 0:128], ki[d][:, h, :], qd[d][:, h, :], True, True,
                     reads=[("ki", d, h), ("qd", d, h)], writes=[("ps", bC)])
                S.tt(sT[d][:], ps[bC][:, 0:128], M2[d][:], ALU.mult, reads=[("ps", bC), ("M2", d)], writes=[("sT", d)])
            for ci in range(2):
                for d in range(2):
                    bC, bR, bO0, bO1 = bank[d]
                    c = ci if d == 0 else 1 - ci
                    rs = slice(c * 64, (c + 1) * 64)
                    for vc in range(2):
                        bo = bO0 if vc == 0 else bO1
                        v0 = h * GDV + vc * 128
                        S.mm(ps[bo][:, rs], vtb[d][sl][rs, v0:v0 + 128], sT[d][rs, rs], True, False,
                             reads=[("vt", d, sl), ("sT", d)], writes=[("ps", bo)])
                        S.mm(ps[bo][:, rs], stb[d][h][:, vc * 128:(vc + 1) * 128], qd[d][:, h, rs], False, True,
                             reads=[("stb", d, h), ("qd", d, h)], writes=[("ps", bo)])
                    S.mm(ps[bR][:, 0:GDV], kend[d][rs, h * 128:(h + 1) * 128], vtb[d][sl][rs, h * GDV:(h + 1) * GDV], True, True,
                         reads=[("kend", d), ("vt", d, sl)], writes=[("ps", bR)])
                    S.stt(st[d][h][:], st[d][h][:], ee[d][h][:, 128 + c:129 + c], ps[bR][:, 0:GDV], ALU.mult, ALU.add,
                          reads=[("st", d, h), ("e", d, h), ("ps", bR)], writes=[("st", d, h)])
                    S.actf(stb[d][h][:], st[d][h][:], AF.Copy, reads=[("st", d, h)], writes=[("stb", d, h)])
            for d in range(2):
                bC, bR, bO0, bO1 = bank[d]
                S.actf(oo[d][osl][:, 2 * h, :], ps[bO0][:, 0:128], AF.Copy, reads=[("ps", bO0)], writes=[("oo", d, osl, 2 * h)])
                S.copy(oo[d][osl][:, 2 * h + 1, :], ps[bO1][:, 0:128], reads=[("ps", bO1)], writes=[("oo", d, osl, 2 * h + 1)])
        for d in range(2):
            t0 = blks[d] * 128
            S.dma("sp", "g_ost%d" % d, o_o[d, :, :, t0:t0 + 128], oo[d][osl][:],
                  reads=[("oo", d, osl, hv) for hv in range(8)], writes=[("oT", d, blks[d])])


def gla_p3(C, cm, ps, Cf, io, wbuf, with_ctx, TM):
    S = C.S
    HT_in, o_i, sr_i, w_o, ogain, HT_out = io["ht_in"], io["oT"], io["srT"], io["w_o"], io["ogain"], io["ht_out"]
    C.out_groups.append("g_hst")
    og = C.sb("og", [128, 8], F32)
    S.dma("sp", "g_og", og[:], ogain, writes=["og"])
    S.dma("pool", "g_w", wbuf[:, :, 0:D], w_o.rearrange("(kc p) f -> p kc f", p=128), writes=["wbuf"])
    of = C.sb("of", [128, 8, TM], F32)
    ob = C.sb("ob", [128, 8, TM], F32)
    sr = C.sb("sr", [128, 8, TM], BF16)
    z = C.sb("z", [128, 8, TM], BF16)
    y = C.sb("y", [128, KC, TM], F32)
    rstdy = C.sb("rstdy", [128, TM], F32)
    ro = [C.sb("ro%d" % i, [128, TM], F32) for i in range(2)]
    if not with_ctx:
        C.out_groups.append("g_cp")
        S.dma("sp", "g_cp0", cm.h[:, :, 0:NCTX], ht_view(HT_in)[:, :, 0:NCTX],
              writes=[("h", kc, s) for kc in range(KC) for s in range(2)])
        S.dma("sp", "g_cp", ht_view(HT_out)[:, :, 0:NCTX], cm.h[:, :, 0:NCTX],
              reads=[("h", kc, s) for kc in range(KC) for s in range(2)], writes=[("HTout", 0)])
    for ti, (t0, T, m) in enumerate(tiles_for(0, with_ctx, TM)):
        S.dma("sp", "g_of", of[:, :, 0:T], o_i[0, :, :, t0:t0 + T], writes=[("of", hv) for hv in range(8)])
        S.dma("sp", "g_ob", ob[:, :, 0:T], o_i[1, :, :, t0:t0 + T], writes=[("ob", hv) for hv in range(8)])
        S.dma("sp", "g_sr", sr[:, :, 0:T], sr_i[:, :, t0:t0 + T], writes=["sr"])
        for h in range(GH):
            b = S.nxt("g3ss", 2)
            for vc in range(2):
                hv = 2 * h + vc
                S.tt(of[:, hv, 0:T], of[:, hv, 0:T], ob[:, hv, 0:T], ALU.add, reads=[("of", hv), ("ob", hv)], writes=[("of", hv)])
                k = S.nxt("sq", 2)
                S.actf(cm.sq[k][:, :T], of[:, hv, 0:T], AF.Square, reads=[("of", hv)], writes=[("sq", k)])
                S.mm(ps[b][:, :T], cm.ones[:], cm.sq[k][:, :T], vc == 0, vc == 1, reads=["ones", ("sq", k)], writes=[("ps", b)])
            ri = S.nxt("ro", 2)
            cm.rms_from_psum_ss(ps[b][:, :T], T, ro[ri][:, 0:T], (("ps", b), ("ro", ri)), GDV)
            for vc in range(2):
                hv = 2 * h + vc
                i = S.nxt("tmp", 2)
                S.tt(cm.tmp[i][:, :T], of[:, hv, 0:T], ro[ri][:, 0:T], ALU.mult, reads=[("of", hv), ("ro", ri)], writes=[("tmp", i)])
                S.stt(z[:, hv, 0:T], cm.tmp[i][:, :T], og[:, hv:hv + 1], sr[:, hv, 0:T], ALU.mult, ALU.mult,
                      reads=[("tmp", i), "og", "sr"], writes=[("z", hv)])
        for dc in range(KC):
            b = 2 + S.nxt("g3y", 4)
            for hv in range(8):
                S.mm(ps[b][:, :T], wbuf[:, hv, dc * 128:(dc + 1) * 128], z[:, hv, 0:T], hv == 0, hv == 7,
                     reads=["wbuf", ("z", hv)], writes=[("ps", b)])
            S.actf(y[:, dc, 0:T], ps[b][:, :T], AF.Copy, reads=[("ps", b)], writes=[("y", dc, 0)])
            k = S.nxt("sq", 2)
            S.actf(cm.sq[k][:, :T], ps[b][:, :T], AF.Square, reads=[("ps", b)], writes=[("sq", k)])
            S.mm(ps[6][:, :T], cm.ones[:], cm.sq[k][:, :T], dc == 0, dc == KC - 1, reads=["ones", ("sq", k)], writes=[("ps", 6)])
        cm.rms_from_psum_ss(ps[6][:, :T], T, rstdy[:, 0:T], (("ps", 6), ("rstdy", 0)), D)
        cm.load_h(HT_in, t0, T)
        cm.postnorm_residual_store(y, rstdy, T, m, Cf, HT_out, t0)

def prog_prep():
    C = Ctx()
    S = C.S
    x = C.din("x", [NX, D])
    cx = C.din("ctx", [NCTX, D])
    cin = C.din("cin", [128, KC, 9])
    ada_w = C.din("ada_w", [4, D, 1152])
    ada_b = C.din("ada_b", [4, 128, 9])
    ident = C.din("ident", [128, 128])
    HT = C.dout("ht_out", [D, NTOK])
    modT = C.dout("modp", [4, 128, 9, 9])
    C.out_groups += ["g_hst", "g_modst"]
    ps = C.psum_banks()
    idsb = C.sb("idsb", [128, 128], F32)
    S.dma("sp", "g_id", idsb[:], ident, writes=["ident"])
    xin = [C.sb("xin%d" % i, [128, 4, D], F32) for i in range(2)]
    hT = [C.sb("hT%d" % i, [128, KC, 512], F32) for i in range(2)]
    groups = [(cx, 0, 2, 0)] + [(x, g * 512, 4, NCTX + g * 512) for g in range(8)]
    for gi, (src, r0, nb, t0) in enumerate(groups):
        xi = S.nxt("xin", 2)
        S.dma("sp", "g_xin%d" % xi, xin[xi][:, 0:nb, :], src[r0:r0 + nb * 128, :].rearrange("(b p) d -> p b d", p=128),
              writes=[("xin", xi)])
        hi = S.nxt("hT", 2)
        for kc in range(KC):
            for b in range(nb):
                S.tr(ps[kc][:, b * 128:(b + 1) * 128], xin[xi][:, b, kc * 128:(kc + 1) * 128], idsb[:],
                     reads=[("xin", xi), "ident"], writes=[("ps", kc)])
            if kc % 2 == 0:
                S.actf(hT[hi][:, kc, 0:nb * 128], ps[kc][:, 0:nb * 128], AF.Copy, reads=[("ps", kc)], writes=[("hT", hi, kc)])
            else:
                S.copy(hT[hi][:, kc, 0:nb * 128], ps[kc][:, 0:nb * 128], reads=[("ps", kc)], writes=[("hT", hi, kc)])
        S.dma("sp", "g_hst", ht_view(HT)[:, :, t0:t0 + nb * 128], hT[hi][:, :, 0:nb * 128],
              reads=[("hT", hi, kc) for kc in range(KC)], writes=[("HT", gi)])
    csb = C.sb("csb", [128, KC, 9], F32)
    sc = C.sb("sc", [128, KC, 9], F32)
    S.dma("sp", "g_cin", csb[:], cin, writes=["csb"])
    S.actf(sc[:], csb[:], AF.Silu, reads=["csb"], writes=["sc"])
    aw = [C.sb("aw%d" % i, [128, KC, 1152], F32) for i in range(2)]
    adab = [C.sb("adab%d" % i, [128, 9], F32) for i in range(2)]
    modsb = [C.sb("modo%d" % i, [128, 9, 9], F32) for i in range(2)]
    for l in range(4):
        pb = l % 2
        S.dma("sp", "g_adab%d" % pb, adab[pb][:], ada_b[l], writes=[("adab", pb)])
        S.dma("sp", "g_aw%d" % pb, aw[pb][:], ada_w[l].rearrange("(kc p) f -> p kc f", p=128), writes=[("aw", pb)])
        for f in range(9):
            for kc in range(KC):
                S.mm(ps[pb][:, 9 * f:9 * f + 9], aw[pb][:, kc, f * 128:(f + 1) * 128], sc[:, kc, :], kc == 0, kc == KC - 1,
                     reads=[("aw", pb), "sc"], writes=[("psm", pb)])
        for f in range(9):
            S.ts(modsb[pb][:, f, :], ps[pb][:, 9 * f:9 * f + 9], adab[pb][:, f:f + 1], None, ALU.add, ALU.bypass,
                 reads=[("psm", pb), ("adab", pb)], writes=[("modo", pb, f)])
        S.dma("sp", "g_modst", modT[l], modsb[pb][:], reads=[("modo", pb, f) for f in range(9)], writes=[("modT", l)])
    return C.finish()


def prog_final():
    C = Ctx()
    S = C.S
    HT = C.din("ht_in", [D, NTOK])
    ident = C.din("ident", [128, 128])
    out = C.dout("out", [NX, D])
    C.out_groups += ["g_ost"]
    ps = C.psum_banks()
    idsb = C.sb("idsb", [128, 128], F32)
    S.dma("sp", "g_id", idsb[:], ident, writes=["ident"])
    hT = [C.sb("hT%d" % i, [128, KC, 512], F32) for i in range(2)]
    xo = [C.sb("xo%d" % i, [128, 4, D], F32) for i in range(2)]
    for g in range(8):
        t0 = NCTX + g * 512
        hi = S.nxt("hT", 2)
        S.dma("sp", "g_hin%d" % hi, hT[hi][:], ht_view(HT)[:, :, t0:t0 + 512], writes=[("hT", hi)])
        xi = S.nxt("xo", 2)
        for b in range(4):
            for half in range(2):
                bank = (b % 4) * 2 + half
                for k4 in range(4):
                    kc = half * 4 + k4
                    S.tr(ps[bank][:, k4 * 128:(k4 + 1) * 128], hT[hi][:, kc, b * 128:(b + 1) * 128], idsb[:],
                         reads=[("hT", hi), "ident"], writes=[("ps", bank)])
                if half == 0:
                    S.actf(xo[xi][:, b, 0:512], ps[bank][:], AF.Copy, reads=[("ps", bank)], writes=[("xo", xi, b, half)])
                else:
                    S.copy(xo[xi][:, b, 512:1024], ps[bank][:], reads=[("ps", bank)], writes=[("xo", xi, b, half)])
        S.dma("sp", "g_ost", out[g * 512:(g + 1) * 512, :].rearrange("(b p) d -> p b d", p=128), xo[xi][:],
              reads=[("xo", xi, b, h2) for b in range(4) for h2 in range(2)], writes=[("out", g)])
    return C.finish()


_PROGS = {}


def _prog(key, fn, *a):
    if key not in _PROGS:
        _PROGS[key] = fn(*a)
    return _PROGS[key]


def _launch(nc, in_maps):
    res = run_bass_kernel_spmd(nc, in_maps, core_ids=list(range(NCORES)))
    return res.results


def _col(v):
    return np.ascontiguousarray(np.asarray(v, np.float32).reshape(KC, 128).T)


def kernel(x, c, ctx, c_ctx, ada_w, ada_b, norm_pre, norm_post, ffn_w_in, ffn_w_out,
           attn_w_qkv, attn_q_gain, attn_k_gain, attn_w_o,
           gla_w_in, gla_wa1, gla_wa2, gla_ba, gla_o_gain, gla_w_o, _stop_after=None, _state=None):
    f32 = lambda a: np.ascontiguousarray(np.asarray(a, dtype=np.float32))
    x, c, ctx, c_ctx = f32(x), f32(c), f32(ctx), f32(c_ctx)
    ada_w, ada_b = f32(ada_w), f32(ada_b)
    norm_pre, norm_post = f32(norm_pre), f32(norm_post)
    ffn_w_in, ffn_w_out = f32(ffn_w_in), f32(ffn_w_out)
    ident = np.eye(128, dtype=np.float32)
    cin = np.ascontiguousarray(np.stack([_col(c[b]) for b in range(NCORES)] + [_col(c_ctx)], axis=-1))
    in_maps = []
    for k in range(NCORES):
        awk = np.ascontiguousarray(ada_w[:, :, k * 1152:(k + 1) * 1152])
        abk = np.ascontiguousarray(ada_b[:, k * 1152:(k + 1) * 1152].reshape(4, 9, 128).transpose(0, 2, 1))
        in_maps.append({"x": x[k], "ctx": ctx[k], "cin": cin, "ada_w": awk, "ada_b": abk, "ident": ident})
    if _state is None:
        r = _launch(_prog("prep", prog_prep), in_maps)
        HT = [r[b]["ht_out"] for b in range(NCORES)]
        full = np.concatenate([r[k]["modp"] for k in range(NCORES)], axis=2)
        modT = [np.ascontiguousarray(np.stack([full[..., b], full[..., 8]], axis=2)) for b in range(NCORES)]
        k0 = 0
    else:
        HT, modT, k0 = _state

    def ffn(i, j, with_ctx):
        nonlocal HT
        nc = _prog(("ffn", j, with_ctx), prog_ffn, j, with_ctx)
        maps = [{"ht_in": HT[b], "w_in": ffn_w_in[i, j // 2], "w_out": ffn_w_out[i, j // 2],
                 "modT": np.ascontiguousarray(modT[b][i]), "gpre": _col(norm_pre[i, j]), "gpost": _col(norm_post[i, j])}
                for b in range(NCORES)]
        rr = _launch(nc, maps)
        HT = [rr[b]["ht_out"] for b in range(NCORES)]

    cosT, sinT, RTm = rope_consts()

    def attn(i):
        nonlocal HT
        mi = i // 2
        nc = _prog("attn", prog_attn)
        qkg = np.ascontiguousarray(np.stack([f32(attn_q_gain)[mi], f32(attn_k_gain)[mi]], axis=-1))
        maps = [{"ht_in": HT[b], "w_qkv": f32(attn_w_qkv)[mi], "w_o": f32(attn_w_o)[mi],
                 "modT": np.ascontiguousarray(modT[b][i]), "gpre": _col(norm_pre[i, 1]), "gpost": _col(norm_post[i, 1]),
                 "qkg": qkg, "cosT": cosT, "sinT": sinT, "RT": RTm} for b in range(NCORES)]
        rr = _launch(nc, maps)
        HT = [rr[b]["ht_out"] for b in range(NCORES)]

    gcst = gla_consts()

    def gla(i):
        nonlocal HT
        mi = i // 2
        last = i == 3
        wa2a = np.zeros((2, 33, 512), np.float32)
        wa2a[:, 0:16, :] = f32(gla_wa2)[mi]
        wa2a[:, 32, :] = f32(gla_ba)[mi]
        maps = [dict(gcst, ht_in=HT[b], w_in=f32(gla_w_in)[mi], wa1=f32(gla_wa1)[mi], wa2a=wa2a,
                     w_o=f32(gla_w_o)[mi], ogain=_col(f32(gla_o_gain)[mi]),
                     modT=np.ascontiguousarray(modT[b][i]), gpre=_col(norm_pre[i, 1]), gpost=_col(norm_post[i, 1]))
                for b in range(NCORES)]
        r3 = _launch(_prog(("gla", not last), prog_gla, not last), maps)
        HT = [r3[b]["ht_out"] for b in range(NCORES)]

    plan = []
    for i in range(4):
        last = i == 3
        plan += [("ffn", i, 0, True), ("mix", i), ("ffn", i, 2, not last)]
    for k, st in enumerate(plan):
        if k < k0:
            continue
        if _stop_after is not None and k >= _stop_after:
            break
        if st[0] == "ffn":
            ffn(st[1], st[2], st[3])
        elif st[1] % 2 == 0:
            attn(st[1])
        else:
            gla(st[1])
        if _stop_after is not None:
            import pickle
            pickle.dump((HT, modT), open("_state_%d.pkl" % (k + 1), "wb"))
    if _stop_after is not None:
        return HT, modT
    rr = _launch(_prog("final", prog_final), [{"ht_in": HT[b], "ident": ident} for b in range(NCORES)])
    return np.stack([rr[b]["out"] for b in range(NCORES)], axis=0)
```
